# Optimizing a Trainium2 kernel written in Bass

```python
import functools
import jax, jax.numpy as jnp
from jax import lax
import numpy as np

D_MODEL = 1024
BATCH = 8
SEQ = 4096
DEPTH = 1
DEC_BATCH = 128
DEC_SEQ = 4
PAST_LEN = 16384
PAGE_SIZE = 128

HEAD_DIM = 64
N_HEADS = 8
N_KV_HEADS = 2
GROUP = N_HEADS // N_KV_HEADS
ATTN_DIM = N_HEADS * HEAD_DIM
KV_DIM = N_KV_HEADS * HEAD_DIM
WINDOW = 128
BLOCK = WINDOW
ROPE_THETA = 500000.0
ROPE_DIM = HEAD_DIM // 4
CONV_DIM = D_MODEL // 2
CONV_WIDTH = 31
FFN_DIM = -(-8 * D_MODEL // (3 * 256)) * 256
RMS_EPS = 1e-6
LN_EPS = 1e-5
NEG_INF = -1e30
IN_SIZES = (ATTN_DIM, KV_DIM, KV_DIM, CONV_DIM, CONV_DIM, D_MODEL, D_MODEL)
IN_COLS = sum(IN_SIZES)
IN_SPLITS = tuple(int(s) for s in np.cumsum(IN_SIZES)[:-1])

kernel_name = "hybrid_conformer_conv_swa_sink_decoder_step"


def rms_norm(x, g):
    xf = x.astype(jnp.float32)
    y = xf * lax.rsqrt(jnp.mean(xf * xf, axis=-1, keepdims=True) + RMS_EPS)
    return (y * g.astype(jnp.float32)).astype(x.dtype)


def adaln_mods(c, w_ada, b_ada):
    mods = jax.nn.silu(c) @ w_ada + b_ada
    return jnp.split(mods[:, None, :], 6, axis=-1)


def partial_rope(x, pos):
    half = ROPE_DIM // 2
    inv = ROPE_THETA ** (-jnp.arange(0, ROPE_DIM, 2, dtype=jnp.float32) / ROPE_DIM)
    ang = pos[:, None] * inv[None, :]
    cos = jnp.cos(ang)[:, None, :]
    sin = jnp.sin(ang)[:, None, :]
    xf = x.astype(jnp.float32)
    x1 = xf[..., :half]
    x2 = xf[..., half:ROPE_DIM]
    out = jnp.concatenate([x1 * cos - x2 * sin, x2 * cos + x1 * sin, xf[..., ROPE_DIM:]], axis=-1)
    return out.astype(x.dtype)


def sink_softmax(s, mask, sink):
    s = jnp.where(mask, s, NEG_INF)
    m = jnp.maximum(jnp.max(s, axis=-1, keepdims=True), sink)
    p = jnp.exp(s - m)
    den = jnp.sum(p, axis=-1, keepdims=True) + jnp.exp(sink - m)
    return p / den


def project_mixer_inputs(h, w_in, pos):
    B, T, _ = h.shape
    z = h @ w_in
    q, k, v, glu_a, glu_b, g_att, g_conv = jnp.split(z, IN_SPLITS, axis=-1)
    q = partial_rope(q.reshape(B, T, N_HEADS, HEAD_DIM), pos)
    k = partial_rope(k.reshape(B, T, N_KV_HEADS, HEAD_DIM), pos)
    v = v.reshape(B, T, N_KV_HEADS, HEAD_DIM)
    u = glu_a * jax.nn.sigmoid(glu_b)
    return q, k, v, u, g_att, g_conv


def swa_prompt(q, k, v, sinks):
    B, T = q.shape[:2]
    nb = T // BLOCK
    scale = HEAD_DIM ** -0.5
    qb = q.reshape(B, nb, BLOCK, N_KV_HEADS, GROUP, HEAD_DIM)
    pad = jnp.zeros((B, BLOCK, N_KV_HEADS, HEAD_DIM), k.dtype)
    kp = jnp.concatenate([pad, k], axis=1).reshape(B, nb + 1, BLOCK, N_KV_HEADS, HEAD_DIM)
    vp = jnp.concatenate([pad, v], axis=1).reshape(B, nb + 1, BLOCK, N_KV_HEADS, HEAD_DIM)
    kb = jnp.concatenate([kp[:, :-1], kp[:, 1:]], axis=2)
    vb = jnp.concatenate([vp[:, :-1], vp[:, 1:]], axis=2)
    s = jnp.einsum('bnqkgd,bnskd->bnkgqs', qb, kb, preferred_element_type=jnp.float32) * scale
    blk = jnp.arange(nb)[:, None] * BLOCK
    qpos = blk + jnp.arange(BLOCK)[None, :]
    kpos = blk - BLOCK + jnp.arange(2 * BLOCK)[None, :]
    rel = qpos[:, :, None] - kpos[:, None, :]
    mask = (rel >= 0) & (rel < WINDOW) & (kpos[:, None, :] >= 0)
    mask = mask[None, :, None, None]
    sink = sinks.astype(jnp.float32).reshape(1, 1, N_KV_HEADS, GROUP, 1, 1)
    probs = sink_softmax(s, mask, sink)
    o = jnp.einsum('bnkgqs,bnskd->bnqkgd', probs.astype(v.dtype), vb)
    return o.reshape(B, T, ATTN_DIM)


def swa_sample(q, k_new, v_new, k_buf, v_buf, sinks):
    Bd, S = q.shape[:2]
    L = k_buf.shape[1]
    scale = HEAD_DIM ** -0.5
    kc = jnp.concatenate([k_buf, k_new], axis=1)
    vc = jnp.concatenate([v_buf, v_new], axis=1)
    qg = q.reshape(Bd, S, N_KV_HEADS, GROUP, HEAD_DIM)
    s = jnp.einsum('btkgd,bskd->bkgts', qg, kc, preferred_element_type=jnp.float32) * scale
    qpos = PAST_LEN + jnp.arange(S)
    kpos = PAST_LEN - L + jnp.arange(L + S)
    rel = qpos[:, None] - kpos[None, :]
    mask = ((rel >= 0) & (rel < WINDOW))[None, None, None]
    sink = sinks.astype(jnp.float32).reshape(1, N_KV_HEADS, GROUP, 1, 1)
    probs = sink_softmax(s, mask, sink)
    o = jnp.einsum('bkgts,bskd->btkgd', probs.astype(vc.dtype), vc)
    return o.reshape(Bd, S, ATTN_DIM), kc[:, -L:], vc[:, -L:]


def conv_branch(u_hist, conv_w, conv_b, ln_g, ln_b, w_conv_o):
    dw = lax.conv_general_dilated(u_hist, conv_w[:, None, :], window_strides=(1,), padding='VALID',
                                  dimension_numbers=('NWC', 'WIO', 'NWC'),
                                  feature_group_count=CONV_DIM) + conv_b
    df = dw.astype(jnp.float32)
    mu = jnp.mean(df, axis=-1, keepdims=True)
    var = jnp.mean(jnp.square(df - mu), axis=-1, keepdims=True)
    n = ((df - mu) * lax.rsqrt(var + LN_EPS) * ln_g.astype(jnp.float32) + ln_b.astype(jnp.float32)).astype(dw.dtype)
    return jax.nn.silu(n) @ w_conv_o


def merge_branches(a, cv_out, g_att, g_conv, lw):
    a_out = a @ lw['w_attn_o']
    m = jax.nn.sigmoid(g_att) * a_out + jax.nn.sigmoid(g_conv) * cv_out
    return m @ lw['w_out']


def mixer_prompt(h, lw):
    B, T, _ = h.shape
    pos = jnp.arange(T, dtype=jnp.float32)
    q, k, v, u, g_att, g_conv = project_mixer_inputs(h, lw['w_in'], pos)
    a = swa_prompt(q, k, v, lw['sinks'])
    u_hist = jnp.concatenate([jnp.zeros((B, CONV_WIDTH - 1, CONV_DIM), u.dtype), u], axis=1)
    cv_out = conv_branch(u_hist, lw['conv_w'], lw['conv_b'], lw['conv_ln_g'], lw['conv_ln_b'], lw['w_conv_o'])
    out = merge_branches(a, cv_out, g_att, g_conv, lw)
    L = min(WINDOW, T)
    return out, (k[:, -L:], v[:, -L:], u[:, -(CONV_WIDTH - 1):])


def mixer_sample(h, k_buf, v_buf, conv_buf, lw):
    T = h.shape[1]
    pos = PAST_LEN + jnp.arange(T, dtype=jnp.float32)
    q, k, v, u, g_att, g_conv = project_mixer_inputs(h, lw['w_in'], pos)
    a, k_new_buf, v_new_buf = swa_sample(q, k, v, k_buf, v_buf, lw['sinks'])
    u_hist = jnp.concatenate([conv_buf, u], axis=1)
    cv_out = conv_branch(u_hist, lw['conv_w'], lw['conv_b'], lw['conv_ln_g'], lw['conv_ln_b'], lw['w_conv_o'])
    out = merge_branches(a, cv_out, g_att, g_conv, lw)
    return out, (k_new_buf, v_new_buf, u_hist[:, -(CONV_WIDTH - 1):])


def swiglu(h, w_ffn_in, w_ffn_out):
    g, u = jnp.split(h @ w_ffn_in, 2, axis=-1)
    return (jax.nn.silu(g) * u) @ w_ffn_out


def trunk_layer(x, c, lw, mixer):
    sh1, sc1, g1, sh2, sc2, g2 = adaln_mods(c, lw['w_ada'], lw['b_ada'])
    h = rms_norm(x, lw['norm1_g']) * (1 + sc1) + sh1
    m, states = mixer(h)
    x = x + g1 * m
    h2 = rms_norm(x, lw['norm2_g']) * (1 + sc2) + sh2
    x = x + g2 * swiglu(h2, lw['w_ffn_in'], lw['w_ffn_out'])
    return x, states


def setup_inputs(seed: int = 0) -> dict:
    key = jax.random.key(seed)
    ks = jax.random.split(key, 24)
    f32 = jnp.float32
    nrm = lambda k, shape, s: jax.random.normal(k, shape, f32) * s
    win_rows = min(WINDOW, PAST_LEN)
    return {
        'x_prompt': nrm(ks[0], (BATCH, SEQ, D_MODEL), 1.0),
        'x_sample': nrm(ks[1], (DEC_BATCH, DEC_SEQ, D_MODEL), 1.0),
        'state_k_win': nrm(ks[2], (DEPTH, DEC_BATCH, win_rows, N_KV_HEADS, HEAD_DIM), 1.0),
        'state_v_win': nrm(ks[3], (DEPTH, DEC_BATCH, win_rows, N_KV_HEADS, HEAD_DIM), 1.0),
        'state_conv': nrm(ks[4], (DEPTH, DEC_BATCH, CONV_WIDTH - 1, CONV_DIM), 0.5),
        'c_prompt': nrm(ks[5], (BATCH, D_MODEL), 1.0),
        'c_sample': nrm(ks[6], (DEC_BATCH, D_MODEL), 1.0),
        'norm1_g': 1.0 + nrm(ks[7], (DEPTH, D_MODEL), 0.01),
        'norm2_g': 1.0 + nrm(ks[8], (DEPTH, D_MODEL), 0.01),
        'w_ada': nrm(ks[9], (DEPTH, D_MODEL, 6 * D_MODEL), 0.5 * D_MODEL ** -0.5),
        'b_ada': nrm(ks[10], (DEPTH, 6 * D_MODEL), 0.01),
        'w_in': nrm(ks[11], (DEPTH, D_MODEL, IN_COLS), D_MODEL ** -0.5),
        'sinks': nrm(ks[12], (DEPTH, N_HEADS), 1.0),
        'w_attn_o': nrm(ks[13], (DEPTH, ATTN_DIM, D_MODEL), ATTN_DIM ** -0.5),
        'conv_w': nrm(ks[14], (DEPTH, CONV_WIDTH, CONV_DIM), CONV_WIDTH ** -0.5),
        'conv_b': nrm(ks[15], (DEPTH, CONV_DIM), 0.01),
        'conv_ln_g': 1.0 + nrm(ks[16], (DEPTH, CONV_DIM), 0.01),
        'conv_ln_b': nrm(ks[17], (DEPTH, CONV_DIM), 0.01),
        'w_conv_o': nrm(ks[18], (DEPTH, CONV_DIM, D_MODEL), CONV_DIM ** -0.5),
        'w_out': nrm(ks[19], (DEPTH, D_MODEL, D_MODEL), D_MODEL ** -0.5),
        'w_ffn_in': nrm(ks[20], (DEPTH, D_MODEL, 2 * FFN_DIM), D_MODEL ** -0.5),
        'w_ffn_out': nrm(ks[21], (DEPTH, FFN_DIM, D_MODEL), FFN_DIM ** -0.5),
        'final_norm_g': 1.0 + nrm(ks[22], (D_MODEL,), 0.01),
    }


def reference(x_prompt, x_sample, state_k_win, state_v_win, state_conv, c_prompt, c_sample,
              norm1_g, norm2_g, w_ada, b_ada, w_in, sinks, w_attn_o, conv_w, conv_b,
              conv_ln_g, conv_ln_b, w_conv_o, w_out, w_ffn_in, w_ffn_out, final_norm_g):
    xp, xs = x_prompt, x_sample
    kp_l, vp_l, cp_l, ks_l, vs_l, cs_l = [], [], [], [], [], []
    for l in range(DEPTH):
        lw = dict(norm1_g=norm1_g[l], norm2_g=norm2_g[l], w_ada=w_ada[l], b_ada=b_ada[l],
                  w_in=w_in[l], sinks=sinks[l], w_attn_o=w_attn_o[l], conv_w=conv_w[l],
                  conv_b=conv_b[l], conv_ln_g=conv_ln_g[l], conv_ln_b=conv_ln_b[l],
                  w_conv_o=w_conv_o[l], w_out=w_out[l], w_ffn_in=w_ffn_in[l], w_ffn_out=w_ffn_out[l])
        xp, (kp, vp, cp) = trunk_layer(xp, c_prompt, lw, functools.partial(mixer_prompt, lw=lw))
        xs, (ksn, vsn, csn) = trunk_layer(
            xs, c_sample, lw,
            functools.partial(mixer_sample, k_buf=state_k_win[l], v_buf=state_v_win[l],
                              conv_buf=state_conv[l], lw=lw))
        kp_l.append(kp); vp_l.append(vp); cp_l.append(cp)
        ks_l.append(ksn); vs_l.append(vsn); cs_l.append(csn)
    y_prompt = rms_norm(xp, final_norm_g)
    y_sample = rms_norm(xs, final_norm_g)
    k_win_prompt = jnp.stack(kp_l)
    v_win_prompt = jnp.stack(vp_l)
    conv_prompt = jnp.stack(cp_l)
    k_win_sample = jnp.stack(ks_l)
    v_win_sample = jnp.stack(vs_l)
    conv_sample = jnp.stack(cs_l)
    return (y_prompt, y_sample, k_win_prompt, v_win_prompt, conv_prompt, k_win_sample, v_win_sample, conv_sample)
```

```python
import contextlib
import os
import numpy as np
import concourse.bass as bass
import concourse.mybir as mybir
from concourse.bass_utils import run_bass_kernel_spmd

F32 = mybir.dt.float32
BF16 = mybir.dt.bfloat16
ALU = mybir.AluOpType
AF = mybir.ActivationFunctionType

D = 1024
SEQ = 4096
NCORE = 8
SB = 16
ST = 4
NS_TOK = SB * ST
PAST = 16384
HD = 64
FFN = 2816
NHC = FFN // 128
CW = 31
TT = 512
NTILE = SEQ // TT
RMS_EPS = 1e-6
LN_EPS = 1e-5
SCALE = HD ** -0.5
NSLOT = 8
KSTOP = int(os.environ.get('KSTOP', '99'))
KSUB = int(os.environ.get('KSUB', '99'))
SLOT = 2048

V_N1G, V_N2G, V_BSH1, V_BSC1, V_BSH2, V_BSC2, V_CB, V_LG, V_LB, V_CWT = 0, 8, 16, 24, 32, 40, 48, 52, 56, 60
NVEC = 60 + CW * 4

C_QP = [256 * j for j in range(4)]
C_KP = 1024
C_V = 1280
C_GLU = [1408 + 256 * c for c in range(4)]
C_GATT = [2432 + 256 * p for p in range(4)]
C_GCONV = [3456 + 256 * p for p in range(4)]
NEXT = 4480


class Sched:
    ENGS = ("pe", "act", "dve", "pool", "sp")

    def __init__(self, nc, stack):
        self.nc = nc
        self.stack = stack
        self.eng = {"pe": nc.tensor, "act": nc.scalar, "dve": nc.vector,
                    "pool": nc.gpsimd, "sp": nc.sync}
        self.sems = {}
        self.count = {}
        self.waited = {e: {} for e in self.ENGS}
        self.last_w = {}
        self.readers = {}
        self.nops = {e: 0 for e in self.ENGS}
        self.pending = {e: {} for e in self.ENGS}
        self.dma_res_w = {}
        self.dma_res_r = {}
        self.n_wait = 0
        self.n_ins = 0

    def _sem(self, key):
        if key not in self.sems:
            self.sems[key] = self.stack.enter_context(self.nc.semaphore("s_" + key))
            self.count[key] = 0
        return self.sems[key]

    def _collect(self, eng, reads, writes, is_dma):
        need = dict(self.pending[eng])
        self.pending[eng] = {}

        def add(kind, tok):
            skey, val, peng, pidx = tok
            if (not is_dma) and peng == eng:
                if eng == "pe":
                    return
            if need.get(skey, 0) < val:
                need[skey] = val

        for r in reads:
            t = self.last_w.get(r)
            if t is not None:
                add("raw", t)
        for w in writes:
            t = self.last_w.get(w)
            if t is not None:
                add("waw", t)
            for t in self.readers.get(w, {}).values():
                add("war", t)
        return need

    def _do_waits(self, eng, need):
        E = self.eng[eng]
        wd = self.waited[eng]
        for skey, val in need.items():
            if wd.get(skey, 0) >= val:
                continue
            E.wait_ge(self.sems[skey], val)
            wd[skey] = val
            self.n_wait += 1

    def _record(self, tok, reads, writes):
        for w in writes:
            self.last_w[w] = tok
            self.readers[w] = {}
        for r in reads:
            self.readers.setdefault(r, {})[tok[0]] = tok

    def op(self, eng, fn, reads=(), writes=()):
        return self.group(eng, [fn], reads, writes)

    def group(self, eng, fns, reads=(), writes=()):
        need = self._collect(eng, reads, writes, False)
        self._do_waits(eng, need)
        ins = None
        for fn in fns:
            ins = fn()
            self.nops[eng] += 1
            self.n_ins += 1
        sem = self._sem(eng)
        self.count[eng] += 1
        ins.then_inc(sem, 1)
        tok = (eng, self.count[eng], eng, self.nops[eng])
        self._record(tok, reads, writes)
        return tok

    def dma(self, q, key, out, in_, reads=(), writes=(), **kw):
        if key == "cst":
            self.n_uniq = getattr(self, "n_uniq", 0) + 1
            key = "c%d" % self.n_uniq
        need = self._collect(q, reads, writes, True)
        self._do_waits(q, need)
        sem = self._sem(key)
        ins = self.eng[q].dma_start(out=out, in_=in_, **kw)
        self.nops[q] += 1
        self.n_ins += 1
        self.count[key] += 16
        ins.then_inc(sem, 16)
        tok = (key, self.count[key], "dma", 0)
        rw = self.dma_res_w.setdefault(key, set())
        for r in list(rw):
            t = self.last_w.get(r)
            if t is not None and t[0] == key:
                self.last_w[r] = tok
            else:
                rw.discard(r)
        rr = self.dma_res_r.setdefault(key, set())
        for r in list(rr):
            d = self.readers.get(r, {})
            if key in d:
                d[key] = tok
            else:
                rr.discard(r)
        self._record(tok, reads, writes)
        rw.update(writes)
        rr.update(reads)
        return tok

    def barrier(self):
        need = {k: v for k, v in self.count.items() if v > 0}
        for e in self.ENGS:
            p = self.pending[e]
            for k, v in need.items():
                if p.get(k, 0) < v:
                    p[k] = v

    def final_wait(self, eng="sp"):
        need = {k: v for k, v in self.count.items() if v > 0}
        self._do_waits(eng, need)


def build_program():
    nc = bass.Bass("TRN2", target_bir_lowering=False)

    def din(name, shape):
        return nc.dram_tensor(name, list(shape), F32, kind="ExternalInput").ap()

    def dout(name, shape):
        return nc.dram_tensor(name, list(shape), F32, kind="ExternalOutput").ap()

    xp = din("xp", [SEQ, D])
    xs = din("xs", [NS_TOK, D])
    ksin = din("ksin", [SB, 128, 128])
    vsin = din("vsin", [SB, 128, 128])
    csin = din("csin", [SB, 30, 512])
    call = din("call", [1 + SB, D])
    wadaf = din("wadaf", [128, 8 * 4 * D])
    wadat = din("wadat", [D, 2 * D])
    bada = din("bada", [1, 6 * D])
    winx = din("winx", [128, 8 * NEXT])
    wao = din("wao", [64, 8 * D])
    wco = din("wco", [128, 4 * D])
    wout = din("wout", [D, D])
    wfi = din("wfi", [128, 8 * 2 * FFN])
    wfo = din("wfo", [FFN, D])
    vecs = din("vecs", [NVEC, 128])
    fng = din("fng", [1, D])
    sinks = din("sinks", [1, 8])
    ropefm = din("ropefm", [128, 2, SEQ + NS_TOK])
    ropetm = din("ropetm", [128 + NS_TOK, 2, 128])
    masks = din("masks", [128, 256 + 512 + 512])
    identd = din("identd", [128, 128])

    yp = dout("yp", [SEQ, D])
    ys = dout("ys", [NS_TOK, D])
    kwp = dout("kwp", [128, 128])
    vwp = dout("vwp", [128, 128])
    cvp = dout("cvp", [30, 512])
    kws = dout("kws", [SB, 128, 128])
    vws = dout("vws", [SB, 128, 128])
    cvs = dout("cvs", [SB, 30, 512])

    with contextlib.ExitStack() as st:
        S = Sched(nc, st)

        def T(name, shape, dt, stack=st):
            return stack.enter_context(nc.sbuf_tensor(name, list(shape), dt))

        ps = [st.enter_context(nc.psum_tensor("ps%d" % i, [128, 512], F32)) for i in range(8)]
        bank_rr = [0]

        def nb_():
            b = bank_rr[0]
            bank_rr[0] = (b + 1) % 8
            return b

        ring = T("ring", [128, NSLOT, SLOT], BF16)
        hT = T("hT", [128, 8, TT], BF16)
        qTz = [T("qTz0", [128, 4, TT], BF16), T("qTz1", [128, 4, TT], BF16)]
        aT = T("aT", [128, 8, TT], BF16)
        nT = T("nT", [128, 4, TT], BF16)
        mT = T("mT", [128, 8, TT], BF16)
        hidT = T("hidT", [128, NHC, TT], BF16)
        xn = T("xn", [128, 2, D], F32)
        sf = T("sf", [128, 6, 512], F32)
        dwb = T("dwb", [128, 4, 512], F32)
        ident = T("ident", [128, 128], F32)
        onesb = T("onesb", [128, 64], BF16)
        onesdiv = T("onesdiv", [128, 128], F32)
        vecT = T("vecT", [128, NVEC], F32)
        modsT = T("modsT", [128, 4, 8, 17], F32)
        G1p = T("G1p", [128, D], F32)
        G2p = T("G2p", [128, D], F32)
        Gf = T("Gf", [128, D], F32)
        sinkexp = T("sinkexp", [64, 8], F32)
        stt = T("stt", [128, 8, 4], F32)
        epsc = T("epsc", [128, 2], F32)

        sf_rr = [0]

        def sfn():
            i = sf_rr[0]
            sf_rr[0] = (i + 1) % 4
            return i

        st_rr = [0]

        def stn():
            i = st_rr[0]
            st_rr[0] = (i + 1) % 8
            return i

        xn_rr = [0]

        plan = []
        state = {"issued": 0, "next": 0}

        def w_view(s, shape, parts=128):
            n = 1
            for d_ in shape:
                n *= d_
            v = ring[:parts, s, 0:n]
            if len(shape) == 2:
                return v.rearrange("p (a b) -> p a b", a=shape[0])
            return v

        def issue_next():
            i = state["issued"]
            if i >= len(plan):
                return
            name, shape, parts, src = plan[i]
            s = i % NSLOT
            if len(src.shape) == 2:
                n = 1
                for d_ in shape:
                    n *= d_
                dst = ring[:parts, s, 0:n]
            else:
                dst = w_view(s, shape, parts)
            S.dma("pool", "w%d" % s, dst, src, writes=[("w", s)])
            state["issued"] += 1

        def w_get(name):
            i = state["next"]
            assert plan[i][0] == name, (plan[i][0], name)
            assert i < state["issued"], "weight piece not issued: ring too small for live set"
            state["next"] += 1
            s = i % NSLOT
            return s, w_view(s, plan[i][1], plan[i][2])

        def w_done():
            issue_next()

        wadat_v = wadat.rearrange("(kc p) n -> p kc n", p=128)
        wout_v = wout.rearrange("(kc p) n -> p kc n", p=128)
        wfo_v = wfo.rearrange("(kc p) n -> p kc n", p=128)

        def pm(ap, c0, w, nk=8):
            return ap[:, nk * c0:nk * (c0 + w)]

        for mi in range(4):
            for p in range(4):
                plan.append(("ada_fm", (8, 256), 128, pm(wadaf, mi * D + p * 256, 256)))
        for m_idx in range(2):
            for p in range(4):
                plan.append(("ada_tm", (2, 1024), 128, wadat_v[:, 2 * p:2 * p + 2, m_idx * D:(m_idx + 1) * D]))

        def plan_p2():
            for j in range(4):
                plan.append(("qp", (8, 256), 128, pm(winx, C_QP[j], 256)))
            plan.append(("kp", (8, 256), 128, pm(winx, C_KP, 256)))
            plan.append(("v", (8, 128), 128, pm(winx, C_V, 128)))
            for c in range(4):
                plan.append(("glu", (8, 256), 128, pm(winx, C_GLU[c], 256)))

        def plan_merge():
            for p in range(4):
                plan.append(("gatt", (8, 256), 128, pm(winx, C_GATT[p], 256)))
                plan.append(("gconv", (8, 256), 128, pm(winx, C_GCONV[p], 256)))
                plan.append(("ao", (8, 256), 64, pm(wao, p * 256, 256)))
                plan.append(("co", (4, 256), 128, pm(wco, p * 256, 256, nk=4)))
            for hf in range(2):
                for p in range(4):
                    plan.append(("wo", (2, 512), 128, wout_v[:, 2 * p:2 * p + 2, hf * 512:(hf + 1) * 512]))

        def plan_ffn():
            for hp in range(NHC // 2):
                plan.append(("fg", (8, 256), 128, pm(wfi, hp * 256, 256)))
                plan.append(("fu", (8, 256), 128, pm(wfi, FFN + hp * 256, 256)))
            for hf in range(2):
                for p in range(NHC // 2):
                    plan.append(("fo", (2, 512), 128, wfo_v[:, 2 * p:2 * p + 2, hf * 512:(hf + 1) * 512]))

        plan_p2(); plan_merge(); plan_ffn()
        plan_p2()
        for ti_ in range(NTILE):
            plan_merge()
            if ti_ + 1 < NTILE:
                plan_p2()
            plan_ffn()

        S.dma("sp", "cst", ident[:], identd, writes=["ident"])
        msk32 = sf[:, 0:3, :].rearrange("p a b -> p (a b)")[:, 0:1280]
        S.dma("sp", "cst", msk32, masks, writes=[("sf", 0), ("sf", 1), ("sf", 2)])
        S.dma("sp", "cst", Gf[:], fng.partition_broadcast(128), writes=["Gf"])
        S.dma("sp", "cst", sinkexp[:], sinks.partition_broadcast(64), writes=["sinkexp"])
        for _ in range(NSLOT):
            issue_next()
        S.op("dve", lambda: nc.vector.memset(onesb[:], 1.0), writes=["onesb"])
        S.op("dve", lambda: nc.vector.memset(qTz[0][:], 0.0), writes=["qT"])
        S.op("dve", lambda: nc.vector.memset(qTz[1][:], 0.0), writes=["qT"])
        S.op("dve", lambda: nc.vector.memset(aT[:], 0.0), writes=["aT"])
        S.op("dve", lambda: nc.vector.memset(onesdiv[:], 1.0 / 512), writes=["onesdiv"])
        S.op("dve", lambda: nc.vector.memset(epsc[:, 0:1], RMS_EPS), writes=["epsc"])
        S.op("dve", lambda: nc.vector.memset(epsc[:, 1:2], LN_EPS), writes=["epsc"])
        S.op("act", lambda: nc.scalar.activation(out=sinkexp[:], in_=sinkexp[:], func=AF.Exp), reads=["sinkexp"], writes=["sinkexp"])

        vrows = sf[:, 3, 0:128]
        vrows2 = sf[:, 4, 0:128]
        S.dma("sp", "cst", vrows, vecs[0:128, :], writes=[("sf", 3)])
        S.dma("sp", "cst", vrows2[0:NVEC - 128, :], vecs[128:NVEC, :], writes=[("sf", 4)])
        b = nb_()
        S.group("pe", [lambda: nc.tensor.transpose(ps[b][:, 0:128], vrows, ident[:]),
                       lambda: nc.tensor.transpose(ps[b][:, 128:NVEC], vrows2[0:NVEC - 128, :], ident[0:NVEC - 128, 0:NVEC - 128])],
                reads=[("sf", 3), ("sf", 4), "ident"], writes=[("ps", b)])
        S.op("dve", lambda: nc.vector.tensor_copy(out=vecT[:], in_=ps[b][:, 0:NVEC]), writes=[("ps", b), "vecT"])

        with contextlib.ExitStack() as stA:
            mk = T("mk", [128, 256 + 512 + 512], BF16, stA)
            S.op("dve", lambda: nc.vector.tensor_copy(out=mk[:], in_=msk32), reads=[("sf", 0), ("sf", 1), ("sf", 2)], writes=["mk"])
            maskSb = mk[:, 256:768]
            maskSn = mk[0:64, 768:1280]
            cs_t = T("cs_t", [1 + SB, D], F32, stA)
            scT = T("scT", [128, 8, 1 + SB], BF16, stA)
            scTp = T("scTp", [128, 8, 128], BF16, stA)
            scTs = T("scTs", [128, 8, NS_TOK], BF16, stA)
            G1s = T("G1s", [NS_TOK, D], F32, stA)
            G2s = T("G2s", [NS_TOK, D], F32, stA)
            xs_t = T("xs_t", [NS_TOK, 1, D], F32, stA)
            ksb = T("ksb", [128, SB, 128], F32, stA)
            vsb = T("vsb", [128, SB, 128], F32, stA)
            kbT = T("kbT", [128, SB, 128], BF16, stA)
            vsb16 = T("vsb16", [128, SB, 128], BF16, stA)
            csb = T("csb", [120, 4, 512], F32, stA)
            uh = T("uh", [128, 4, SB, 34], F32, stA)
            PTb = T("PTb", [128, 512], BF16, stA)
            PTn = T("PTn", [64, 512], BF16, stA)
            kTs = T("kTs", [128, NS_TOK], BF16, stA)
            vnew16 = T("vnew16", [NS_TOK, 128], BF16, stA)
            tabS = T("tabS", [128, 2, NS_TOK], F32, stA)
            tabtS = T("tabtS", [NS_TOK, 2, 128], F32, stA)
            otok = T("otok", [NS_TOK, 128 + 128 + 512], F32, stA)
            qs2z = [T("qs2z0", [128, SB, 4, ST], BF16, stA), T("qs2z1", [128, SB, 4, ST], BF16, stA)]

            S.dma("sp", "cst", cs_t[:], call, writes=["cs_t"])
            S.dma("sp", "xs", xs_t[:, 0, :], xs, writes=[("x", 0)])
            S.dma("sp", "cst", ksb[:], ksin.rearrange("b k f -> k b f"), writes=["ksb"])
            S.dma("sp", "cst", vsb[:], vsin.rearrange("b k f -> k b f"), writes=["vsb"])
            S.dma("sp", "cst", csb[:], csin.rearrange("(rb bl) s f -> (bl s) rb f", bl=4), writes=["csb"])
            S.dma("sp", "cst", tabS[:], ropefm[:, :, SEQ:SEQ + NS_TOK], writes=["tab"])
            S.dma("sp", "cst", tabtS[:], ropetm[128:128 + NS_TOK, :, :], writes=["tabt"])
            S.dma("sp", "o", kws[:, 0:124, :], ksin[:, 4:128, :])
            S.dma("sp", "o", vws[:, 0:124, :], vsin[:, 4:128, :])
            S.dma("sp", "o", cvs[:, 0:26, :], csin[:, 4:30, :])

            S.op("act", lambda: nc.scalar.activation(out=cs_t[:], in_=cs_t[:], func=AF.Silu), reads=["cs_t"], writes=["cs_t"])
            for half in range(2):
                b = nb_()
                S.group("pe", [lambda c=c: nc.tensor.transpose(ps[b][:, (c % 4) * 32:(c % 4) * 32 + 17],
                                                               cs_t[:, c * 128:(c + 1) * 128], ident[0:17, 0:17])
                               for c in range(half * 4, half * 4 + 4)],
                        reads=["cs_t", "ident"], writes=[("ps", b)])
                S.op("dve", lambda half=half, b=b: nc.vector.tensor_copy(
                    out=scT[:, half * 4:half * 4 + 4, :],
                    in_=ps[b][:, 0:128].rearrange("p (a c) -> p a c", a=4)[:, :, 0:17]),
                    writes=[("ps", b), "scT"])
            S.op("dve", lambda: nc.vector.tensor_copy(out=scTp[:], in_=scT[:, :, 0:1].broadcast_to([128, 8, 128])),
                 reads=["scT"], writes=["scTp"])
            for kc in range(8):
                S.op("dve", lambda kc=kc: nc.vector.tensor_copy(
                    out=scTs[:, kc, :].rearrange("p (b t) -> p b t", t=ST),
                    in_=scT[:, kc, 1:1 + SB].unsqueeze(2).broadcast_to([128, SB, ST])),
                    reads=["scT"], writes=["scTs"])

            vbase = [V_BSH1, V_BSC1, V_BSH2, V_BSC2]
            for mi in range(4):
                for p in range(4):
                    s, wv = w_get("ada_fm")
                    for cl in range(2):
                        c = 2 * p + cl
                        b = nb_()
                        S.group("pe", [lambda kc=kc, cl=cl, b=b: nc.tensor.matmul(
                            ps[b][:, 0:17], wv[:, kc, cl * 128:(cl + 1) * 128], scT[:, kc, :], start=(kc == 0), stop=(kc == 7))
                            for kc in range(8)], reads=[("w", s), "scT"], writes=[("ps", b)])
                        S.op("dve", lambda mi=mi, c=c, b=b: nc.vector.tensor_scalar(
                            out=modsT[:, mi, c, :], in0=ps[b][:, 0:17], scalar1=vecT[:, vbase[mi] + c:vbase[mi] + c + 1],
                            scalar2=None, op0=ALU.add), reads=["vecT"], writes=[("ps", b), "modsT"])
                    w_done()
            for mi, vg in ((1, V_N1G), (3, V_N2G)):
                for c in range(8):
                    S.op("dve", lambda mi=mi, vg=vg, c=c: nc.vector.tensor_scalar(
                        out=modsT[:, mi, c, :], in0=modsT[:, mi, c, :], scalar1=1.0, scalar2=vecT[:, vg + c:vg + c + 1],
                        op0=ALU.add, op1=ALU.mult), reads=["modsT", "vecT"], writes=["modsT"])
            for m, Gp, Gs in ((2, G1p, G1s), (5, G2p, G2s)):
                bb = [nb_() for _ in range(4)]
                bbc = sf[:, 4:6, :].rearrange("p a b -> p (a b)")
                S.dma("sp", "cst", bbc, bada[:, m * D:(m + 1) * D].partition_broadcast(128), writes=[("sf", 4), ("sf", 5)])
                for p in range(4):
                    s, wv = w_get("ada_tm")
                    fns = []
                    for kl in range(2):
                        kc = 2 * p + kl
                        for hf in range(2):
                            fns.append(lambda kc=kc, kl=kl, hf=hf: nc.tensor.matmul(
                                ps[bb[hf]][:, :], scTp[:, kc, :], wv[:, kl, hf * 512:(hf + 1) * 512], start=(kc == 0), stop=(kc == 7)))
                            fns.append(lambda kc=kc, kl=kl, hf=hf: nc.tensor.matmul(
                                ps[bb[2 + hf]][0:NS_TOK, :], scTs[:, kc, :], wv[:, kl, hf * 512:(hf + 1) * 512], start=(kc == 0), stop=(kc == 7)))
                    S.group("pe", fns, reads=[("w", s), "scTp", "scTs"], writes=[("ps", x) for x in bb])
                    w_done()
                for hf in range(2):
                    S.op("dve", lambda hf=hf, Gp=Gp: nc.vector.tensor_tensor(
                        out=Gp[:, hf * 512:(hf + 1) * 512], in0=ps[bb[hf]][:, :], in1=bbc[:, hf * 512:(hf + 1) * 512], op=ALU.add),
                        reads=[("sf", 4), ("sf", 5)], writes=[("ps", bb[hf]), "G"])
                    S.op("dve", lambda hf=hf, Gs=Gs: nc.vector.tensor_tensor(
                        out=Gs[:, hf * 512:(hf + 1) * 512], in0=ps[bb[2 + hf]][0:NS_TOK, :], in1=bbc[0:NS_TOK, hf * 512:(hf + 1) * 512], op=ALU.add),
                        reads=[("sf", 4), ("sf", 5)], writes=[("ps", bb[2 + hf]), "G"])

            if KSTOP <= 1:
                S.final_wait("sp")
                return nc
            cur = {"h": hT, "hk": "hT"}

            def norm_gen(kind, xsrc, nb, bp, mi_a, mi_b, xkey="x", hdst=None, hkey="hT"):
                hd = hT if hdst is None else hdst
                info = {}

                def stA(b):
                    xb = xsrc(b)
                    xi = xn_rr[0]
                    xn_rr[0] ^= 1
                    k = stn()
                    info[b] = xi
                    S.op("dve", lambda: nc.vector.memset(stt[:, k, :], 0.0), writes=[("st", k)])
                    S.op("act", lambda: nc.scalar.activation(out=xn[:bp, xi, :], in_=xb, func=AF.Square, accum_out=stt[:bp, k, 0:1]),
                         reads=[(xkey, b)], writes=[("xn", xi), ("st", k)])
                    S.op("act", lambda: nc.scalar.activation(out=stt[:bp, k, 1:2], in_=stt[:bp, k, 0:1], func=AF.Ln, scale=1.0 / D, bias=epsc[:bp, 0:1]),
                         reads=[("st", k), "epsc"], writes=[("st", k)])
                    S.op("act", lambda: nc.scalar.activation(out=stt[:bp, k, 2:3], in_=stt[:bp, k, 1:2], func=AF.Exp, scale=-0.5),
                         reads=[("st", k)], writes=[("st", k)])
                    S.op("dve", lambda: nc.vector.tensor_scalar(out=xn[:bp, xi, :], in0=xb, scalar1=stt[:bp, k, 2:3], scalar2=None, op0=ALU.mult),
                         reads=[(xkey, b), ("st", k)], writes=[("xn", xi)])

                def stB(b):
                    xi = info[b]
                    pbs = []
                    for half in range(2):
                        pb = nb_()
                        pbs.append(pb)
                        S.group("pe", [lambda c=c: nc.tensor.transpose(
                            ps[pb][:, (c % 4) * 128:(c % 4) * 128 + bp], xn[:bp, xi, c * 128:(c + 1) * 128], ident[:bp, :bp])
                            for c in range(half * 4, half * 4 + 4)], reads=[("xn", xi), "ident"], writes=[("ps", pb)])
                    for half in range(2):
                        pb = pbs[half]
                        for c in range(half * 4, half * 4 + 4):
                            src = ps[pb][:, (c % 4) * 128:(c % 4) * 128 + bp]
                            dst = hd[:, c, b * bp:(b + 1) * bp]
                            if kind == "p":
                                if half == 0:
                                    S.op("act", lambda: nc.scalar.activation(
                                        out=dst, in_=src, func=AF.Identity, scale=modsT[:, mi_a, c, 0:1], bias=modsT[:, mi_b, c, 0:1]),
                                        reads=["modsT"], writes=[("ps", pb), (hkey, 0)])
                                else:
                                    S.op("dve", lambda: nc.vector.tensor_scalar(
                                        out=dst, in0=src, scalar1=modsT[:, mi_a, c, 0:1], scalar2=modsT[:, mi_b, c, 0:1], op0=ALU.mult, op1=ALU.add),
                                        reads=["modsT"], writes=[("ps", pb), (hkey, 1)])
                            else:
                                i = sfn()
                                tmp = sf[:, i, 0:NS_TOK]
                                S.op("dve", lambda: nc.vector.tensor_tensor(
                                    out=tmp.rearrange("p (b t) -> p b t", t=ST), in0=src.rearrange("p (b t) -> p b t", t=ST),
                                    in1=modsT[:, mi_a, c, 1:1 + SB].unsqueeze(2).broadcast_to([128, SB, ST]), op=ALU.mult),
                                    reads=["modsT"], writes=[("ps", pb), ("sf", i)])
                                S.op("dve", lambda: nc.vector.tensor_tensor(
                                    out=dst.rearrange("p (b t) -> p b t", t=ST), in0=tmp.rearrange("p (b t) -> p b t", t=ST),
                                    in1=modsT[:, mi_b, c, 1:1 + SB].unsqueeze(2).broadcast_to([128, SB, ST]), op=ALU.add),
                                    reads=["modsT", ("sf", i)], writes=[(hkey, half)])

                stA(0)
                yield
                for b in range(nb):
                    if b + 1 < nb:
                        stA(b + 1)
                        yield
                    stB(b)
                    yield

            def norm_to_hT(*args, **kw):
                for _ in norm_gen(*args, **kw):
                    pass

            def fm_mm(bank, wv, s, col0, src, nk, nt, kp=128, extra_reads=()):
                S.group("pe", [lambda k=k: nc.tensor.matmul(ps[bank][:, 0:nt], wv[:kp, k, col0:col0 + 128], src[:kp, k, 0:nt],
                                                           start=(k == 0), stop=(k == nk - 1)) for k in range(nk)],
                        reads=[("w", s)] + list(extra_reads), writes=[("ps", bank)])

            def tm_mm(bank, c0, wv, s, kidx, col0, ncol, tsl, M, start, stop, extra_reads=()):
                S.group("pe", [lambda k=k: nc.tensor.matmul(ps[bank][:M, c0:c0 + ncol], cur["h"][:, k, tsl], wv[:, k, col0:col0 + ncol],
                                                           start=(start and k == kidx[0]), stop=(stop and k == kidx[-1])) for k in kidx],
                        reads=[("w", s), (cur["hk"], 0), (cur["hk"], 1)] + list(extra_reads), writes=[("ps", bank)])

            def phase2(kind, nt, tabC, tabS_, kT_dst, v_evac, u_dst, tok_extra, side=None):
                for j in range(5):
                    s, wv = w_get("qp" if j < 4 else "kp")
                    b0, b1 = nb_(), nb_()
                    fm_mm(b0, wv, s, 0, cur["h"], 8, nt, extra_reads=[(cur["hk"], 0), (cur["hk"], 1)])
                    fm_mm(b1, wv, s, 128, cur["h"], 8, nt, extra_reads=[(cur["hk"], 0), (cur["hk"], 1)])
                    i0, i1 = sfn(), sfn()
                    S.op("dve", lambda i0=i0, b0=b0: nc.vector.tensor_tensor(out=sf[:, i0, 0:nt], in0=ps[b0][:, 0:nt], in1=tabC, op=ALU.mult),
                         reads=["tab"], writes=[("ps", b0), ("sf", i0)])
                    S.op("dve", lambda i1=i1, b1=b1: nc.vector.tensor_tensor(out=sf[:, i1, 0:nt], in0=ps[b1][:, 0:nt], in1=tabS_, op=ALU.mult),
                         reads=["tab"], writes=[("ps", b1), ("sf", i1)])
                    if j < 4:
                        for g in range(2):
                            gs_ = slice(g * 64, (g + 1) * 64)
                            S.op("dve", lambda i0=i0, i1=i1, g=g, gs_=gs_: nc.vector.tensor_tensor(
                                out=qTz[g][gs_, j, 0:nt], in0=sf[gs_, i0, 0:nt], in1=sf[gs_, i1, 0:nt], op=ALU.add),
                                reads=[("sf", i0), ("sf", i1)], writes=["qT"])
                    else:
                        S.op("dve", lambda i0=i0, i1=i1: nc.vector.tensor_tensor(out=kT_dst, in0=sf[:, i0, 0:nt], in1=sf[:, i1, 0:nt], op=ALU.add),
                             reads=[("sf", i0), ("sf", i1)], writes=["kTcur"])
                    if j == 4 and tok_extra is not None:
                        tsl, M, tc_, ts_, kdst = tok_extra["tsl"], tok_extra["M"], tok_extra["tabtC"], tok_extra["tabtS"], tok_extra["k"]
                        bk = nb_()
                        tm_mm(bk, 0, wv, s, list(range(8)), 0, 256, tsl, M, True, True)
                        i0, i1 = sfn(), sfn()
                        S.op("dve", lambda: nc.vector.tensor_tensor(out=sf[:M, i0, 0:128], in0=ps[bk][:M, 0:128], in1=tc_, op=ALU.mult),
                             reads=["tabt"], writes=[("ps", bk), ("sf", i0)])
                        S.op("dve", lambda: nc.vector.tensor_tensor(out=sf[:M, i1, 0:128], in0=ps[bk][:M, 128:256], in1=ts_, op=ALU.mult),
                             reads=["tabt"], writes=[("ps", bk), ("sf", i1)])
                        S.op("dve", lambda: nc.vector.tensor_tensor(out=kdst, in0=sf[:M, i0, 0:128], in1=sf[:M, i1, 0:128], op=ALU.add),
                             reads=[("sf", i0), ("sf", i1)], writes=[tok_extra["okey"]])
                    w_done()
                    if side is not None:
                        next(side, None)
                s, wv = w_get("v")
                v_evac(s, wv)
                w_done()
                if side is not None:
                    next(side, None)
                for c in range(4):
                    s, wv = w_get("glu")
                    b0, b1 = nb_(), nb_()
                    fm_mm(b0, wv, s, 0, cur["h"], 8, nt, extra_reads=[(cur["hk"], 0), (cur["hk"], 1)])
                    fm_mm(b1, wv, s, 128, cur["h"], 8, nt, extra_reads=[(cur["hk"], 0), (cur["hk"], 1)])
                    i1 = sfn()
                    S.op("act", lambda i1=i1, b1=b1: nc.scalar.activation(out=sf[:, i1, 0:nt], in_=ps[b1][:, 0:nt], func=AF.Sigmoid),
                         writes=[("ps", b1), ("sf", i1)])
                    u_dst(c, b0, i1)
                    if tok_extra is not None:
                        tsl, M, udst = tok_extra["tsl"], tok_extra["M"], tok_extra["u"]
                        bk = nb_()
                        tm_mm(bk, 0, wv, s, list(range(8)), 0, 256, tsl, M, True, True)
                        i2 = sfn()
                        S.op("act", lambda: nc.scalar.activation(out=sf[:M, i2, 0:128], in_=ps[bk][:M, 128:256], func=AF.Sigmoid),
                             writes=[("ps", bk), ("sf", i2)])
                        S.op("dve", lambda c=c: nc.vector.tensor_tensor(out=udst[:, c * 128:(c + 1) * 128], in0=ps[bk][:M, 0:128], in1=sf[:M, i2, 0:128], op=ALU.mult),
                             reads=[("sf", i2)], writes=[("ps", bk), tok_extra["okey"]])
                    w_done()
                    if side is not None:
                        next(side, None)
                if side is not None:
                    for _ in side:
                        pass

            def conv_taps(nt, usrc, accv):
                for j in range(CW):
                    for c in range(4):
                        wcol = vecT[:, V_CWT + j * 4 + c:V_CWT + j * 4 + c + 1]
                        if j == 0:
                            S.op("dve", lambda c=c, wcol=wcol: nc.vector.tensor_scalar(
                                out=accv(c), in0=usrc(c, 0), scalar1=wcol, scalar2=vecT[:, V_CB + c:V_CB + c + 1], op0=ALU.mult, op1=ALU.add),
                                reads=["u", "vecT"], writes=[("dwb", c)])
                        else:
                            S.op("dve", lambda c=c, j=j, wcol=wcol: nc.vector.scalar_tensor_tensor(
                                out=accv(c), in0=usrc(c, j), scalar=wcol, in1=accv(c), op0=ALU.mult, op1=ALU.add),
                                reads=["u", "vecT", ("dwb", c)], writes=[("dwb", c)])
                    yield j

            def ln_gen(nt):
                bm, be = nb_(), nb_()
                S.group("pe", [lambda c=c: nc.tensor.matmul(ps[bm][:, 0:nt], onesdiv[:], dwb[:, c, 0:nt], start=(c == 0), stop=(c == 3)) for c in range(4)],
                        reads=["onesdiv"] + [("dwb", c) for c in range(4)], writes=[("ps", bm)])
                sq = []
                for c in range(4):
                    i = sfn()
                    sq.append(i)
                    S.op("act", lambda: nc.scalar.activation(out=sf[:, i, 0:nt], in_=dwb[:, c, 0:nt], func=AF.Square),
                         reads=[("dwb", c)], writes=[("sf", i)])
                S.group("pe", [lambda c=c: nc.tensor.matmul(ps[be][:, 0:nt], onesdiv[:], sf[:, sq[c], 0:nt], start=(c == 0), stop=(c == 3)) for c in range(4)],
                        reads=["onesdiv"] + [("sf", i) for i in sq], writes=[("ps", be)])
                im, ir = 4, 5
                i2 = sfn()
                S.op("act", lambda: nc.scalar.copy(out=sf[:, im, 0:nt], in_=ps[bm][:, 0:nt]), writes=[("ps", bm), ("sf", im)])
                S.op("act", lambda: nc.scalar.activation(out=sf[:, i2, 0:nt], in_=ps[bm][:, 0:nt], func=AF.Square), writes=[("ps", bm), ("sf", i2)])
                S.op("dve", lambda: nc.vector.tensor_tensor(out=sf[:, ir, 0:nt], in0=ps[be][:, 0:nt], in1=sf[:, i2, 0:nt], op=ALU.subtract),
                     reads=[("sf", i2)], writes=[("ps", be), ("sf", ir)])
                S.op("act", lambda: nc.scalar.activation(out=sf[:, ir, 0:nt], in_=sf[:, ir, 0:nt], func=AF.Ln, bias=epsc[:, 1:2]),
                     reads=[("sf", ir), "epsc"], writes=[("sf", ir)])
                S.op("act", lambda: nc.scalar.activation(out=sf[:, ir, 0:nt], in_=sf[:, ir, 0:nt], func=AF.Exp, scale=-0.5),
                     reads=[("sf", ir)], writes=[("sf", ir)])
                yield
                for c in range(4):
                    S.op("dve", lambda: nc.vector.tensor_tensor(out=dwb[:, c, 0:nt], in0=dwb[:, c, 0:nt], in1=sf[:, im, 0:nt], op=ALU.subtract),
                         reads=[("dwb", c), ("sf", im)], writes=[("dwb", c)])
                    S.op("dve", lambda: nc.vector.tensor_tensor(out=dwb[:, c, 0:nt], in0=dwb[:, c, 0:nt], in1=sf[:, ir, 0:nt], op=ALU.mult),
                         reads=[("dwb", c), ("sf", ir)], writes=[("dwb", c)])
                    S.op("act", lambda: nc.scalar.activation(out=nT[:, c, 0:nt], in_=dwb[:, c, 0:nt], func=AF.Silu,
                                                             scale=vecT[:, V_LG + c:V_LG + c + 1], bias=vecT[:, V_LB + c:V_LB + c + 1]),
                         reads=[("dwb", c), "vecT"], writes=["nT"])
                    yield

            def ln_part(nt):
                for _ in ln_gen(nt):
                    pass

            def merge(nt, side=None):
                for p in range(4):
                    sg, wg = w_get("gatt")
                    sc_, wc = w_get("gconv")
                    sa, wa = w_get("ao")
                    so, wo_ = w_get("co")
                    for cl in range(2):
                        dc = 2 * p + cl
                        ba, bc, bg, bh = nb_(), nb_(), nb_(), nb_()
                        fm_mm(ba, ring[:, sa, 0:2048].rearrange("p (a b) -> p a b", a=8), sa, cl * 128, aT, 8, nt, extra_reads=["aT"])
                        fm_mm(bc, wo_, so, cl * 128, nT, 4, nt, extra_reads=["nT"])
                        fm_mm(bg, wg, sg, cl * 128, cur["h"], 8, nt, extra_reads=[(cur["hk"], 0), (cur["hk"], 1)])
                        fm_mm(bh, wc, sc_, cl * 128, cur["h"], 8, nt, extra_reads=[(cur["hk"], 0), (cur["hk"], 1)])
                        ig, ih, i1, i2 = sfn(), sfn(), sfn(), sfn()
                        S.op("act", lambda: nc.scalar.activation(out=sf[:, ig, 0:nt], in_=ps[bg][:, 0:nt], func=AF.Sigmoid), writes=[("ps", bg), ("sf", ig)])
                        S.op("act", lambda: nc.scalar.activation(out=sf[:, ih, 0:nt], in_=ps[bh][:, 0:nt], func=AF.Sigmoid), writes=[("ps", bh), ("sf", ih)])
                        S.op("dve", lambda: nc.vector.tensor_tensor(out=sf[:, i1, 0:nt], in0=ps[ba][:, 0:nt], in1=sf[:, ig, 0:nt], op=ALU.mult),
                             reads=[("sf", ig)], writes=[("ps", ba), ("sf", i1)])
                        S.op("dve", lambda: nc.vector.tensor_tensor(out=sf[:, i2, 0:nt], in0=ps[bc][:, 0:nt], in1=sf[:, ih, 0:nt], op=ALU.mult),
                             reads=[("sf", ih)], writes=[("ps", bc), ("sf", i2)])
                        S.op("dve", lambda dc=dc: nc.vector.tensor_tensor(out=mT[:, dc, 0:nt], in0=sf[:, i1, 0:nt], in1=sf[:, i2, 0:nt], op=ALU.add),
                             reads=[("sf", i1), ("sf", i2)], writes=["mT"])
                        if side is not None:
                            next(side, None)
                    for _ in range(4):
                        w_done()
                if side is not None:
                    for _ in side:
                        pass

            def tm_proj_resid(nb, bp, src, srckey, wname, npiece, xdst, G, xkey="x", side=None):
                nk = 2 * npiece
                for hf in range(2):
                    banks = [hf * nb + b for b in range(nb)]
                    bank_rr[0] = 4 if hf == 0 else 0
                    for p in range(npiece):
                        s, wv = w_get(wname)
                        fns = []
                        for kl in range(2):
                            kc = 2 * p + kl
                            for b in range(nb):
                                fns.append(lambda kc=kc, kl=kl, b=b: nc.tensor.matmul(
                                    ps[banks[b]][:bp, :], src[:, kc, b * bp:(b + 1) * bp], wv[:, kl, :],
                                    start=(kc == 0), stop=(kc == nk - 1)))
                        S.group("pe", fns, reads=[("w", s), srckey], writes=[("ps", x) for x in banks])
                        w_done()
                        if side is not None:
                            next(side, None)
                    for b in range(nb):
                        i = sfn()
                        S.op("dve", lambda b=b, i=i: nc.vector.tensor_tensor(
                            out=sf[:bp, i, :], in0=ps[banks[b]][:bp, :], in1=G[:bp, hf * 512:(hf + 1) * 512], op=ALU.mult),
                            reads=["G"], writes=[("ps", banks[b]), ("sf", i)])
                        S.op("pool", lambda b=b, i=i: nc.gpsimd.tensor_tensor(
                            out=xdst(b)[:, hf * 512:(hf + 1) * 512], in0=xdst(b)[:, hf * 512:(hf + 1) * 512], in1=sf[:bp, i, :], op=ALU.add),
                            reads=[(xkey, b), ("sf", i)], writes=[(xkey, b)])
                    bank_rr[0] = 0
                if side is not None:
                    for _ in side:
                        pass
                bank_rr[0] = 0

            def ffn_in(nt, hsrc=None, hkey="hT", side=None):
                hs = hT if hsrc is None else hsrc
                for hp in range(NHC // 2):
                    sg, wg = w_get("fg")
                    su, wu = w_get("fu")
                    for hl in range(2):
                        hc = 2 * hp + hl
                        bg, bu = nb_(), nb_()
                        fm_mm(bg, wg, sg, hl * 128, hs, 8, nt, extra_reads=[(hkey, 0), (hkey, 1)])
                        fm_mm(bu, wu, su, hl * 128, hs, 8, nt, extra_reads=[(hkey, 0), (hkey, 1)])
                        i = sfn()
                        S.op("act", lambda: nc.scalar.activation(out=sf[:, i, 0:nt], in_=ps[bg][:, 0:nt], func=AF.Silu), writes=[("ps", bg), ("sf", i)])
                        S.op("dve", lambda hc=hc: nc.vector.tensor_tensor(out=hidT[:, hc, 0:nt], in0=ps[bu][:, 0:nt], in1=sf[:, i, 0:nt], op=ALU.mult),
                             reads=[("sf", i)], writes=[("ps", bu), "hidT"])
                        if side is not None:
                            next(side, None)
                    w_done()
                    w_done()

            def final_gen(nb, bp, xdst, xkey="x", after_block=None):
                ks = {}

                def stA(b):
                    xb = xdst(b)
                    xi = xn_rr[0]
                    xn_rr[0] ^= 1
                    k = stn()
                    ks[b] = k
                    S.op("dve", lambda: nc.vector.memset(stt[:, k, :], 0.0), writes=[("st", k)])
                    S.op("act", lambda: nc.scalar.activation(out=xn[:bp, xi, :], in_=xb, func=AF.Square, accum_out=stt[:bp, k, 0:1]),
                         reads=[(xkey, b)], writes=[("xn", xi), ("st", k)])
                    S.op("act", lambda: nc.scalar.activation(out=stt[:bp, k, 1:2], in_=stt[:bp, k, 0:1], func=AF.Ln, scale=1.0 / D, bias=epsc[:bp, 0:1]),
                         reads=[("st", k), "epsc"], writes=[("st", k)])
                    S.op("act", lambda: nc.scalar.activation(out=stt[:bp, k, 2:3], in_=stt[:bp, k, 1:2], func=AF.Exp, scale=-0.5),
                         reads=[("st", k)], writes=[("st", k)])

                def stB(b):
                    xb = xdst(b)
                    k = ks[b]
                    S.op("dve", lambda: nc.vector.scalar_tensor_tensor(out=xb, in0=xb, scalar=stt[:bp, k, 2:3], in1=Gf[:bp, :], op0=ALU.mult, op1=ALU.mult),
                         reads=[(xkey, b), ("st", k), "Gf"], writes=[(xkey, b)])
                    if after_block is not None:
                        after_block(b)

                stA(0)
                yield
                for b in range(nb):
                    if b + 1 < nb:
                        stA(b + 1)
                    stB(b)
                    yield

            def final_norm(*args, **kw):
                for _ in final_gen(*args, **kw):
                    pass

            def xs_src(b):
                return xs_t[:, 0, :]

            norm_to_hT("s", xs_src, 1, NS_TOK, 1, 0)
            if KSTOP <= 2:
                S.final_wait("sp")
                return nc

            def v_evac_s(s, wv):
                bk = nb_()
                tm_mm(bk, 0, wv, s, list(range(8)), 0, 128, slice(0, NS_TOK), NS_TOK, True, True)
                S.op("act", lambda: nc.scalar.copy(out=vnew16[:], in_=ps[bk][:NS_TOK, 0:128]), writes=[("ps", bk), "vnew16"])
                S.op("dve", lambda: nc.vector.tensor_copy(out=otok[:, 128:256], in_=ps[bk][:NS_TOK, 0:128]), writes=[("ps", bk), "otok"])

            def u_dst_s(c, b0, i1):
                S.op("dve", lambda: nc.vector.tensor_tensor(
                    out=uh[:, c, :, 30:34], in0=ps[b0][:, 0:NS_TOK].rearrange("p (b t) -> p b t", t=ST),
                    in1=sf[:, i1, 0:NS_TOK].rearrange("p (b t) -> p b t", t=ST), op=ALU.mult),
                    reads=[("sf", i1)], writes=[("ps", b0), "u"])

            phase2("s", NS_TOK, tabS[:, 0, :], tabS[:, 1, :], kTs[:], v_evac_s, u_dst_s,
                   dict(tsl=slice(0, NS_TOK), M=NS_TOK, tabtC=tabtS[:, 0, :], tabtS=tabtS[:, 1, :], k=otok[:, 0:128], u=otok[:, 256:768], okey="otok"))
            for bq in range(SB):
                S.dma("sp", "o", kws[bq, 124:128, :], otok[bq * 4:(bq + 1) * 4, 0:128], reads=["otok"])
                S.dma("sp", "o", vws[bq, 124:128, :], otok[bq * 4:(bq + 1) * 4, 128:256], reads=["otok"])
                S.dma("sp", "o", cvs[bq, 26:30, :], otok[bq * 4:(bq + 1) * 4, 256:768], reads=["otok"])

            if KSTOP <= 3:
                S.final_wait("sp")
                return nc
            for grp in range(4):
                pb = nb_()
                S.group("pe", [lambda bq=bq, pb=pb: nc.tensor.transpose(ps[pb][:, (bq % 4) * 128:(bq % 4 + 1) * 128], ksb[:, bq, :], ident[:])
                               for bq in range(grp * 4, grp * 4 + 4)], reads=["ksb", "ident"], writes=[("ps", pb)])
                S.op("act", lambda grp=grp, pb=pb: nc.scalar.copy(out=kbT[:, grp * 4:(grp + 1) * 4, :].rearrange("p a b -> p (a b)"), in_=ps[pb][:, :]),
                     writes=[("ps", pb), "kbT"])
            S.op("dve", lambda: nc.vector.tensor_copy(out=vsb16[:], in_=vsb[:]), reads=["vsb"], writes=["vsb16"])
            if KSUB <= 1:
                S.final_wait("sp")
                return nc
            bX, bY, bZ, bW = nb_(), nb_(), nb_(), nb_()
            for g in range(2):
                S.op("dve", lambda g=g: nc.vector.memset(qs2z[g][:], 0.0), writes=["qs2"])
                for j in range(4):
                    S.op("dve", lambda j=j, g=g: nc.vector.tensor_copy(
                        out=qs2z[g][g * 64:(g + 1) * 64, :, j, :], in_=qTz[g][g * 64:(g + 1) * 64, j, 0:NS_TOK].rearrange("p (b t) -> p b t", t=ST)),
                        reads=["qT"], writes=["qs2"])
            fns = []
            for bq in range(SB):
                for g in range(2):
                    fns.append(lambda bq=bq, g=g: nc.tensor.matmul(
                        ps[bX][:, (bq * 2 + g) * 16:(bq * 2 + g) * 16 + 16],
                        kbT[:, bq, :], qs2z[g][:, bq, :, :].rearrange("p j t -> p (j t)"), start=True, stop=True))
            S.group("pe", fns, reads=["kbT", "qs2"], writes=[("ps", bX)])
            S.group("pe", [lambda g=g: nc.tensor.matmul(
                ps[bY][0:NS_TOK, g * 256:(g + 1) * 256],
                kTs[:, :], qs2z[g][:, :, :, :].rearrange("p b j t -> p (b j t)"),
                start=True, stop=True) for g in range(2)],
                reads=["kTcur", "qs2"], writes=[("ps", bY)])
            S.op("act", lambda: nc.scalar.activation(out=PTb[:], in_=ps[bX][:, :], func=AF.Exp, scale=SCALE), writes=[("ps", bX), "PTb"])
            S.op("act", lambda: nc.scalar.activation(out=PTn[:], in_=ps[bY][0:NS_TOK, :], func=AF.Exp, scale=SCALE), writes=[("ps", bY), "PTn"])
            S.op("dve", lambda: nc.vector.tensor_tensor(out=PTb[:], in0=PTb[:], in1=maskSb, op=ALU.mult), reads=["PTb", "mk"], writes=["PTb"])
            S.op("dve", lambda: nc.vector.tensor_tensor(out=PTn[:], in0=PTn[:], in1=maskSn, op=ALU.mult), reads=["PTn", "mk"], writes=["PTn"])
            if KSUB <= 2:
                S.final_wait("sp")
                return nc
            for bank, use_v in ((bZ, True), (bW, False)):
                fns = []
                for g in range(2):
                    lnew = vnew16[:, g * 64:(g + 1) * 64] if use_v else onesb[0:NS_TOK, :]
                    fns.append(lambda g=g, lnew=lnew, bank=bank: nc.tensor.matmul(
                        ps[bank][0:64, g * 256:(g + 1) * 256], lnew, PTn[:, g * 256:(g + 1) * 256], start=True, stop=True))
                    for bq in range(SB):
                        lb = vsb16[:, bq, g * 64:(g + 1) * 64] if use_v else onesb[:, :]
                        fns.append(lambda g=g, bq=bq, lb=lb, bank=bank: nc.tensor.matmul(
                            ps[bank][0:64, g * 256 + bq * 16:g * 256 + bq * 16 + 16],
                            lb, PTb[:, (bq * 2 + g) * 16:(bq * 2 + g) * 16 + 16],
                            start=False, stop=(bq == SB - 1), skip_group_check=True))
                S.group("pe", fns, reads=["PTb", "PTn", "vnew16", "vsb16", "onesb"], writes=[("ps", bank)])
            if KSUB <= 3:
                S.final_wait("sp")
                return nc
            ird = sfn()
            vj = lambda ap, j: ap.rearrange("p (b j t) -> p b j t", j=4, t=ST)[:, :, j, :]
            for g in range(2):
                for j in range(4):
                    h = g * 4 + j
                    S.op("dve", lambda g=g, j=j, h=h: nc.vector.tensor_scalar(
                        out=vj(sf[0:64, ird, g * 256:(g + 1) * 256], j), in0=vj(ps[bW][0:64, g * 256:(g + 1) * 256], j),
                        scalar1=sinkexp[:, h:h + 1], scalar2=None, op0=ALU.add),
                        reads=["sinkexp"], writes=[("ps", bW), ("sf", ird)])
            S.op("dve", lambda: nc.vector.reciprocal(out=sf[0:64, ird, :], in_=sf[0:64, ird, :]), reads=[("sf", ird)], writes=[("sf", ird)])
            for g in range(2):
                for j in range(4):
                    h = g * 4 + j
                    S.op("dve", lambda g=g, j=j, h=h: nc.vector.tensor_tensor(
                        out=aT[0:64, h, 0:NS_TOK].rearrange("p (b t) -> p b t", t=ST),
                        in0=vj(ps[bZ][0:64, g * 256:(g + 1) * 256], j), in1=vj(sf[0:64, ird, g * 256:(g + 1) * 256], j), op=ALU.mult),
                        reads=[("sf", ird)], writes=[("ps", bZ), "aT"])

            if KSTOP <= 4:
                S.final_wait("sp")
                return nc
            for c in range(4):
                pb = nb_()
                S.group("pe", [lambda rb=rb, c=c, pb=pb: nc.tensor.transpose(ps[pb][:, rb * 120:(rb + 1) * 120], csb[:, rb, c * 128:(c + 1) * 128], ident[0:120, 0:120])
                               for rb in range(4)], reads=["csb", "ident"], writes=[("ps", pb)])
                S.op("act", lambda c=c, pb=pb: nc.scalar.copy(out=uh[:, c, :, 0:30], in_=ps[pb][:, 0:480].rearrange("p (b s) -> p b s", s=30)),
                     writes=[("ps", pb), "u"])
            for _ in conv_taps(NS_TOK, lambda c, j: uh[:, c, :, j:j + ST], lambda c: dwb[:, c, 0:NS_TOK].rearrange("p (b t) -> p b t", t=ST)):
                pass
            ln_part(NS_TOK)
            merge(NS_TOK)
            tm_proj_resid(1, NS_TOK, mT, "mT", "wo", 4, xs_src, G1s)
            norm_to_hT("s", xs_src, 1, NS_TOK, 3, 2)
            ffn_in(NS_TOK)
            tm_proj_resid(1, NS_TOK, hidT, "hidT", "fo", NHC // 2, xs_src, G2s)
            final_norm(1, NS_TOK, xs_src)
            S.dma("sp", "o", ys, xs_t[:, 0, :], reads=[("x", 0)])

        if KSTOP <= 5:
            S.final_wait("sp")
            return nc
        S.barrier()

        with contextlib.ExitStack() as stB:
            xbuf = T("xbuf", [128, 2, 4, D], F32, stB)
            hB = T("hB", [128, 8, TT], BF16, stB)
            hT2 = T("hT2", [128, 8, TT], BF16, stB)
            HA = [(hT, "hT"), (hT2, "hT2")]
            print("sbuf bytes remaining (stage B):", nc.sbuf_bytes_remaining)
            kT = T("kT", [128, 128 + TT], BF16, stB)
            vtok = T("vtok", [128, 5, 128], BF16, stB)
            uT = T("uT", [128, 4, 32 + TT], F32, stB)
            tab = T("tab", [128, 2, TT], F32, stB)
            PT = T("PT", [128, 2, 2, 512], BF16, stB)
            tabtP = T("tabtP", [128, 2, 128], F32, stB)
            sinkrow = T("sinkrow", [128, 8, 128], BF16, stB)
            otokp = hidT[:, 0:3, :].rearrange("p a b -> p (a b)").bitcast(F32)
            assert tuple(otokp.shape) == (128, 768), otokp.shape
            OKEY = "hidT"

            mbias = T("mbias", [128, 2, 4, 128], BF16, stB)
            identb = T("identb", [128, 128], BF16, stB)
            S.dma("sp", "cst", sf[:, 0, 0:256], masks[:, 0:256], writes=[("sf", 0)])
            for hh in range(2):
                S.op("dve", lambda hh=hh: nc.vector.tensor_scalar(
                    out=mbias[:, hh, :, :], in0=sf[:, 0, hh * 128:(hh + 1) * 128].unsqueeze(1).broadcast_to([128, 4, 128]),
                    scalar1=-1.0, scalar2=30000.0, op0=ALU.add, op1=ALU.mult), reads=[("sf", 0)], writes=["mbias"])
            S.op("dve", lambda: nc.vector.tensor_copy(out=identb[:], in_=ident[:]), reads=["ident"], writes=["identb"])
            S.op("dve", lambda: nc.vector.memset(uT[:, :, 0:32], 0.0), writes=["u"])
            S.op("dve", lambda: nc.vector.memset(sinkrow[:], 0.0), writes=["sinkrow"])
            S.op("dve", lambda: nc.vector.tensor_copy(out=sinkrow[0:1, :, :], in_=sinkexp[0:1, :].unsqueeze(2).broadcast_to([1, 8, 128])),
                 reads=["sinkexp"], writes=["sinkrow"])
            S.dma("sp", "cst", tabtP[:], ropetm[0:128, :, :], writes=["tabt"])

            def xkey(ti):
                return ("xb", ti % 2)

            def xsrc_of(ti):
                xb = ti % 2
                return lambda b: xbuf[:, xb, b, :]

            def load_xb(ti, b):
                xb = ti % 2
                r0 = ti * TT + b * 128
                S.dma("sp", "x%d%d" % (xb, b), xbuf[:, xb, b, :], xp[r0:r0 + 128, :], writes=[(xkey(ti), b)])

            def load_x(ti):
                xb = ti % 2
                for b in range(4):
                    load_xb(ti, b)

            def load_tab(ti):
                S.dma("sp", "tb", tab[:, :, :], ropefm[:, :, ti * TT:(ti + 1) * TT], writes=["tab"])

            def P1gen(ti):
                h, hk = HA[ti % 2]
                return norm_gen("p", xsrc_of(ti), 4, 128, 1, 0, xkey=xkey(ti), hdst=h, hkey=hk)

            def set_cur(ti):
                cur["h"], cur["hk"] = HA[ti % 2]

            def P2(ti, side=None):
                last = ti == NTILE - 1
                set_cur(ti)

                def v_evac_p(s, wv):
                    bk = nb_()
                    for b in range(4):
                        tm_mm(bk, b * 128, wv, s, list(range(8)), 0, 128, slice(b * 128, (b + 1) * 128), 128, True, True)
                    S.op("act", lambda: nc.scalar.copy(out=vtok[:, 1:5, :].rearrange("p a b -> p (a b)"), in_=ps[bk][:, :]), writes=[("ps", bk), "vcur"])
                    if last:
                        S.op("dve", lambda: nc.vector.tensor_copy(out=otokp[:, 128:256], in_=ps[bk][:, 384:512]), writes=[("ps", bk), OKEY])

                def u_dst_p(c, b0, i1):
                    S.op("dve", lambda: nc.vector.tensor_tensor(out=uT[:, c, 32:32 + TT], in0=ps[b0][:, :], in1=sf[:, i1, :], op=ALU.mult),
                         reads=[("sf", i1)], writes=[("ps", b0), "u"])

                extra = None
                if last:
                    extra = dict(tsl=slice(384, 512), M=128, tabtC=tabtP[:, 0, :], tabtS=tabtP[:, 1, :], k=otokp[:, 0:128], u=otokp[:, 256:768], okey=OKEY)
                phase2("p", TT, tab[:, 0, :], tab[:, 1, :], kT[:, 128:128 + TT], v_evac_p, u_dst_p, extra, side=side)
                if last:
                    S.dma("sp", "o", kwp, otokp[:, 0:128], reads=[OKEY])
                    S.dma("sp", "o", vwp, otokp[:, 128:256], reads=[OKEY])
                    S.dma("sp", "o", cvp, otokp[98:128, 256:768], reads=[OKEY])
                if ti + 1 < NTILE:
                    load_tab(ti + 1)

            def conv_p():
                return conv_taps(TT, lambda c, j: uT[:, c, 2 + j:2 + j + TT], lambda c: dwb[:, c, :])

            def halo_u():
                S.op("act", lambda: nc.scalar.copy(out=uT[:, :, 0:32], in_=uT[:, :, TT:TT + 32]), reads=["u"], writes=["u"])

            def P3(ti, side=None):
                def stageA(b, g):
                    gb = ti * 4 + b
                    has_prev = gb > 0
                    pbuf = g
                    for hh, kcol in ((1, 128 + b * 128), (0, b * 128)):
                        if hh == 0 and not has_prev:
                            continue
                        bk = nb_()
                        fns = [lambda hh=hh, bk=bk: nc.tensor.matmul(ps[bk][:, :], identb[:, :], mbias[:, hh, :, :].rearrange("p j q -> p (j q)"), start=True, stop=True)]
                        for j in range(4):
                            fns.append(lambda j=j, bk=bk, kcol=kcol: nc.tensor.matmul(
                                ps[bk][:, j * 128:(j + 1) * 128], kT[:, kcol:kcol + 128], qTz[g][:, j, b * 128:(b + 1) * 128],
                                start=False, stop=True, skip_group_check=True))
                        S.group("pe", fns, reads=["qT", "kTcur", "kThalo", "mbias", "identb"], writes=[("ps", bk)])
                        S.op("act", lambda hh=hh, bk=bk: nc.scalar.activation(out=PT[:, pbuf, hh, :], in_=ps[bk][:, :], func=AF.Exp, scale=SCALE),
                             writes=[("ps", bk), ("PT", pbuf, hh)])

                def stageB(b, g):
                    gb = ti * 4 + b
                    has_prev = gb > 0
                    pbuf = g
                    gs = slice(g * 64, (g + 1) * 64)
                    bO, bD = nb_(), nb_()
                    for bank, use_v in ((bO, True), (bD, False)):
                        fns = []
                        lc = vtok[:, 1 + b, gs] if use_v else onesb[:, :]
                        fns.append(lambda lc=lc, bank=bank: nc.tensor.matmul(ps[bank][0:64, :], lc, PT[:, pbuf, 1, :], start=True, stop=(use_v and not has_prev)))
                        if has_prev:
                            lp = vtok[:, b, gs] if use_v else onesb[:, :]
                            fns.append(lambda lp=lp, bank=bank: nc.tensor.matmul(ps[bank][0:64, :], lp, PT[:, pbuf, 0, :], start=False, stop=use_v))
                        if not use_v:
                            fns.append(lambda bank=bank: nc.tensor.matmul(ps[bank][0:64, :], onesb[:, :], sinkrow[:, g * 4:(g + 1) * 4, :].rearrange("p j q -> p (j q)"),
                                                                            start=False, stop=True))
                        S.group("pe", fns, reads=[("PT", pbuf, 0), ("PT", pbuf, 1), "vcur", "vhalo", "onesb", "sinkrow"], writes=[("ps", bank)])
                    ird = sfn()
                    S.op("act", lambda: nc.scalar.activation(out=sf[0:64, ird, :], in_=ps[bD][0:64, :], func=AF.Ln), writes=[("ps", bD), ("sf", ird)])
                    S.op("act", lambda: nc.scalar.activation(out=sf[0:64, ird, :], in_=sf[0:64, ird, :], func=AF.Exp, scale=-1.0), reads=[("sf", ird)], writes=[("sf", ird)])
                    S.op("dve", lambda: nc.vector.tensor_tensor(out=aT[0:64, g * 4:(g + 1) * 4, b * 128:(b + 1) * 128], in0=ps[bO][0:64, :].rearrange("p (j q) -> p j q", j=4),
                                                                in1=sf[0:64, ird, :].rearrange("p (j q) -> p j q", j=4), op=ALU.mult),
                         reads=[("sf", ird)], writes=[("ps", bO), "aT"])

                steps = [(b, g) for b in range(4) for g in range(2)]
                for k, (b, g) in enumerate(steps):
                    stageA(b, g)
                    if k > 0:
                        stageB(*steps[k - 1])
                    if side is not None:
                        next(side, None)
                stageB(*steps[-1])
                if side is not None:
                    for _ in side:
                        pass
                if ti + 1 < NTILE:
                    S.op("act", lambda: nc.scalar.copy(out=kT[:, 0:128], in_=kT[:, TT:TT + 128]), reads=["kTcur"], writes=["kThalo"])
                    S.op("act", lambda: nc.scalar.copy(out=vtok[:, 0, :], in_=vtok[:, 4, :]), reads=["vcur"], writes=["vhalo"])

            def finish_gen(ti):
                def after_block(b):
                    r0 = ti * TT + b * 128
                    S.dma("sp", "y%d%d" % (ti % 2, b), yp[r0:r0 + 128, :], xbuf[:, ti % 2, b, :], reads=[(xkey(ti), b)])
                    if ti + 2 < NTILE:
                        load_xb(ti + 2, b)
                for _ in final_gen(4, 128, xsrc_of(ti), xkey=xkey(ti), after_block=after_block):
                    yield

            load_x(0)
            load_tab(0)
            load_x(1)
            for _ in P1gen(0):
                pass
            P2(0)
            for ti in range(NTILE):
                more = ti + 1 < NTILE
                xk = xkey(ti)
                xs_ = xsrc_of(ti)
                def p3_side(ti=ti):
                    if ti == 0:
                        for j_, _ in enumerate(conv_p()):
                            if j_ % 6 == 5:
                                yield
                        halo_u()
                        yield
                        for _ in ln_gen(TT):
                            yield
                    if ti > 0:
                        for _ in finish_gen(ti - 1):
                            yield
                P3(ti, side=p3_side())
                set_cur(ti)
                merge(TT, side=(P1gen(ti + 1) if more else None))
                tm_proj_resid(4, 128, mT, "mT", "wo", 4, xs_, G1p, xkey=xk)
                n2 = norm_gen("p", xs_, 4, 128, 3, 2, xkey=xk, hdst=hB, hkey="hB")
                side = None
                if more:
                    P2(ti + 1, side=n2)

                    def ffn_side():
                        for _ in conv_p():
                            yield
                        halo_u()
                        for _ in ln_gen(TT):
                            yield
                    side = ffn_side()
                else:
                    for _ in n2:
                        pass
                ffn_in(TT, hsrc=hB, hkey="hB", side=side)
                tm_proj_resid(4, 128, hidT, "hidT", "fo", NHC // 2, xs_, G2p, xkey=xk, side=side)
                if KSTOP <= 6 + ti:
                    S.final_wait("sp")
                    return nc
            for _ in finish_gen(NTILE - 1):
                pass

        S.final_wait("sp")
        print("instructions", S.n_ins, "waits", S.n_wait, "sems", len(S.sems))
    return nc


def _host_consts():
    f32 = np.float32
    half = 8
    inv = (np.float32(500000.0) ** (-np.arange(0, 16, 2, dtype=f32) / f32(16))).astype(f32)
    pos = np.concatenate([np.arange(SEQ, dtype=f32), (PAST + np.tile(np.arange(ST), SB)).astype(f32)])
    ang = (pos[:, None] * inv[None, :]).astype(f32)
    cos, sin = np.cos(ang).astype(f32), np.sin(ang).astype(f32)
    ntok = pos.shape[0]
    Cd = np.ones((ntok, 64), f32)
    Sd = np.zeros((ntok, 64), f32)
    Cd[:, 0:8] = cos
    Cd[:, 8:16] = cos
    Sd[:, 0:8] = -sin
    Sd[:, 8:16] = sin
    C128 = np.concatenate([Cd, Cd], axis=1)
    S128 = np.concatenate([Sd, Sd], axis=1)
    ropefm = np.ascontiguousarray(np.stack([C128.T, S128.T], axis=1))
    tm_rows = np.concatenate([np.arange(SEQ - 128, SEQ), np.arange(SEQ, SEQ + NS_TOK)])
    ropetm = np.ascontiguousarray(np.stack([C128[tm_rows], S128[tm_rows]], axis=1))
    jj = np.arange(128)[:, None]
    ii = np.arange(128)[None, :]
    maskP = (jj > ii).astype(f32)
    maskC = (jj <= ii).astype(f32)
    t4 = np.arange(ST)
    msb = (jj > t4[None, :]).astype(f32)
    maskSb = np.tile(msb, (1, 128))
    kb, kt = np.arange(NS_TOK) // ST, np.arange(NS_TOK) % ST
    msn = ((kb[:, None] == kb[None, :]) & (kt[:, None] <= kt[None, :])).astype(f32)
    maskSn = np.zeros((128, 512), f32)
    m4 = np.broadcast_to(msn.reshape(NS_TOK, SB, 1, ST), (NS_TOK, SB, 4, ST)).reshape(NS_TOK, 256)
    maskSn[0:NS_TOK] = np.tile(m4, (1, 2))
    masks = np.ascontiguousarray(np.concatenate([maskP, maskC, maskSb, maskSn], axis=1))
    return ropefm, ropetm, masks, np.eye(128, dtype=f32)


def _ext_cols():
    def swp(base):
        return list(range(base + 8, base + 16)) + list(range(base, base + 8)) + list(range(base + 16, base + 64))
    cols = []
    for j in range(4):
        cols += list(range(j * 64, (j + 1) * 64)) + list(range((4 + j) * 64, (5 + j) * 64))
        cols += swp(j * 64) + swp((4 + j) * 64)
    cols += list(range(512, 640))
    cols += swp(512) + swp(576)
    cols += list(range(640, 768))
    for c in range(4):
        cols += list(range(768 + c * 128, 768 + (c + 1) * 128))
        cols += list(range(1280 + c * 128, 1280 + (c + 1) * 128))
    cols += list(range(1792, 2816))
    cols += list(range(2816, 3840))
    assert len(cols) == NEXT
    return np.array(cols)


_PROGRAM = None


def kernel(x_prompt, x_sample, state_k_win, state_v_win, state_conv, c_prompt, c_sample,
           norm1_g, norm2_g, w_ada, b_ada, w_in, sinks, w_attn_o, conv_w, conv_b,
           conv_ln_g, conv_ln_b, w_conv_o, w_out, w_ffn_in, w_ffn_out, final_norm_g):
    global _PROGRAM
    f32 = np.float32
    A = lambda a: np.ascontiguousarray(np.asarray(a, dtype=f32))
    if _PROGRAM is None:
        _PROGRAM = build_program()
    nc = _PROGRAM
    ropefm, ropetm, masks, ident = _host_consts()
    def pmaj(w, width_list, nk=8):
        n = w.shape[1]
        r = w.reshape(nk, 128, n).transpose(1, 0, 2)
        out, c0 = [], 0
        for wd in width_list:
            out.append(r[:, :, c0:c0 + wd].reshape(128, nk * wd))
            c0 += wd
        assert c0 == n
        return A(np.concatenate(out, axis=1))

    winx_full = np.asarray(w_in)[0][:, _ext_cols()]
    widths = [256] * 5 + [128] + [256] * 12
    winx = pmaj(winx_full, widths)
    wada_np = np.asarray(w_ada)[0]
    wadaf = pmaj(np.concatenate([wada_np[:, m * D:(m + 1) * D] for m in (0, 1, 3, 4)], axis=1), [256] * 16)
    wadat = A(np.concatenate([wada_np[:, 2 * D:3 * D], wada_np[:, 5 * D:6 * D]], axis=1))
    wfi_pm = pmaj(np.asarray(w_ffn_in)[0], [256] * 22)
    wao_np = np.asarray(w_attn_o)[0].reshape(8, 64, D).transpose(1, 0, 2)
    wao_pm = A(np.concatenate([wao_np[:, :, p * 256:(p + 1) * 256].reshape(64, 8 * 256) for p in range(4)], axis=1))
    wco_np = np.asarray(w_conv_o)[0].reshape(4, 128, D).transpose(1, 0, 2)
    wco_pm = A(np.concatenate([wco_np[:, :, p * 256:(p + 1) * 256].reshape(128, 4 * 256) for p in range(4)], axis=1))
    b_ = np.asarray(b_ada)[0]
    vec_rows = [np.asarray(norm1_g)[0].reshape(8, 128), np.asarray(norm2_g)[0].reshape(8, 128),
                b_[0:D].reshape(8, 128), b_[D:2 * D].reshape(8, 128), b_[3 * D:4 * D].reshape(8, 128), b_[4 * D:5 * D].reshape(8, 128),
                np.asarray(conv_b)[0].reshape(4, 128), np.asarray(conv_ln_g)[0].reshape(4, 128), np.asarray(conv_ln_b)[0].reshape(4, 128),
                np.asarray(conv_w)[0].reshape(CW * 4, 128)]
    vecs = A(np.concatenate(vec_rows, axis=0))
    assert vecs.shape == (NVEC, 128)
    shared = dict(wadaf=wadaf, wadat=wadat, bada=A(b_.reshape(1, -1)), winx=winx, wao=wao_pm,
                  wco=wco_pm, wout=A(np.asarray(w_out)[0]), wfi=wfi_pm,
                  wfo=A(np.asarray(w_ffn_out)[0]), vecs=vecs, fng=A(np.asarray(final_norm_g).reshape(1, D)),
                  sinks=A(np.asarray(sinks).reshape(1, 8)), ropefm=ropefm, ropetm=ropetm, masks=masks, identd=ident)
    xpn, xsn = np.asarray(x_prompt), np.asarray(x_sample)
    skn, svn, scn = np.asarray(state_k_win)[0], np.asarray(state_v_win)[0], np.asarray(state_conv)[0]
    cpn, csn = np.asarray(c_prompt), np.asarray(c_sample)
    in_maps = []
    for i in range(NCORE):
        sl = slice(i * SB, (i + 1) * SB)
        m = dict(shared)
        m.update(xp=A(xpn[i]), xs=A(xsn[sl].reshape(NS_TOK, D)), ksin=A(skn[sl].reshape(SB, 128, 128)),
                 vsin=A(svn[sl].reshape(SB, 128, 128)), csin=A(scn[sl]),
                 call=A(np.concatenate([cpn[i:i + 1], csn[sl]], axis=0)))
        in_maps.append(m)
    res = run_bass_kernel_spmd(nc, in_maps, core_ids=list(range(NCORE)))
    R = res.results
    y_prompt = np.stack([R[i]["yp"] for i in range(NCORE)], axis=0).astype(f32)
    y_sample = np.concatenate([R[i]["ys"].reshape(SB, ST, D) for i in range(NCORE)], axis=0).astype(f32)
    kwp = np.stack([R[i]["kwp"].reshape(128, 2, 64) for i in range(NCORE)], axis=0)[None].astype(f32)
    vwp = np.stack([R[i]["vwp"].reshape(128, 2, 64) for i in range(NCORE)], axis=0)[None].astype(f32)
    cvp = np.stack([R[i]["cvp"] for i in range(NCORE)], axis=0)[None].astype(f32)
    kws = np.concatenate([R[i]["kws"].reshape(SB, 128, 2, 64) for i in range(NCORE)], axis=0)[None].astype(f32)
    vws = np.concatenate([R[i]["vws"].reshape(SB, 128, 2, 64) for i in range(NCORE)], axis=0)[None].astype(f32)
    cvs = np.concatenate([R[i]["cvs"] for i in range(NCORE)], axis=0)[None].astype(f32)
    return (y_prompt, y_sample, kwp, vwp, cvp, kws, vws, cvs)
```

```python
import contextlib
import os
import numpy as np
import concourse.bass as bass
import concourse.mybir as mybir
from concourse.bass_utils import run_bass_kernel_spmd

F32 = mybir.dt.float32
BF16 = mybir.dt.bfloat16
ALU = mybir.AluOpType
AF = mybir.ActivationFunctionType

D = 1024
SEQ = 4096
NCORE = 8
SB = 16
ST = 4
NS_TOK = SB * ST
PAST = 16384
HD = 64
FFN = 2816
NHC = FFN // 128
CW = 31
TT = 512
NTILE = SEQ // TT
RMS_EPS = 1e-6
LN_EPS = 1e-5
SCALE = HD ** -0.5
NSLOT = 8
KSTOP = int(os.environ.get('KSTOP', '99'))
KSUB = int(os.environ.get('KSUB', '99'))
SLOT = 2048

V_N1G, V_N2G, V_BSH1, V_BSC1, V_BSH2, V_BSC2, V_CB, V_LG, V_LB, V_CWT = 0, 8, 16, 24, 32, 40, 48, 52, 56, 60
NVEC = 60 + CW * 4

C_QP = [256 * j for j in range(4)]
C_KP = 1024
C_V = 1280
C_GLU = [1408 + 256 * c for c in range(4)]
C_GATT = [2432 + 256 * p for p in range(4)]
C_GCONV = [3456 + 256 * p for p in range(4)]
NEXT = 4480


class Sched:
    ENGS = ("pe", "act", "dve", "pool", "sp")

    def __init__(self, nc, stack):
        self.nc = nc
        self.stack = stack
        self.eng = {"pe": nc.tensor, "act": nc.scalar, "dve": nc.vector,
                    "pool": nc.gpsimd, "sp": nc.sync}
        self.sems = {}
        self.count = {}
        self.waited = {e: {} for e in self.ENGS}
        self.last_w = {}
        self.readers = {}
        self.nops = {e: 0 for e in self.ENGS}
        self.pending = {e: {} for e in self.ENGS}
        self.dma_res_w = {}
        self.dma_res_r = {}
        self.n_wait = 0
        self.n_ins = 0

    def _sem(self, key):
        if key not in self.sems:
            self.sems[key] = self.stack.enter_context(self.nc.semaphore("s_" + key))
            self.count[key] = 0
        return self.sems[key]

    def _collect(self, eng, reads, writes, is_dma):
        need = dict(self.pending[eng])
        self.pending[eng] = {}

        def add(kind, tok):
            skey, val, peng, pidx = tok
            if (not is_dma) and peng == eng:
                if eng == "pe":
                    return
            if need.get(skey, 0) < val:
                need[skey] = val

        for r in reads:
            t = self.last_w.get(r)
            if t is not None:
                add("raw", t)
        for w in writes:
            t = self.last_w.get(w)
            if t is not None:
                add("waw", t)
            for t in self.readers.get(w, {}).values():
                add("war", t)
        return need

    def _do_waits(self, eng, need):
        E = self.eng[eng]
        wd = self.waited[eng]
        for skey, val in need.items():
            if wd.get(skey, 0) >= val:
                continue
            E.wait_ge(self.sems[skey], val)
            wd[skey] = val
            self.n_wait += 1

    def _record(self, tok, reads, writes):
        for w in writes:
            self.last_w[w] = tok
            self.readers[w] = {}
        for r in reads:
            self.readers.setdefault(r, {})[tok[0]] = tok

    def op(self, eng, fn, reads=(), writes=()):
        return self.group(eng, [fn], reads, writes)

    def group(self, eng, fns, reads=(), writes=()):
        need = self._collect(eng, reads, writes, False)
        self._do_waits(eng, need)
        ins = None
        for fn in fns:
            ins = fn()
            self.nops[eng] += 1
            self.n_ins += 1
        sem = self._sem(eng)
        self.count[eng] += 1
        ins.then_inc(sem, 1)
        tok = (eng, self.count[eng], eng, self.nops[eng])
        self._record(tok, reads, writes)
        return tok

    def dma(self, q, key, out, in_, reads=(), writes=(), **kw):
        if key == "cst":
            self.n_uniq = getattr(self, "n_uniq", 0) + 1
            key = "c%d" % self.n_uniq
        need = self._collect(q, reads, writes, True)
        self._do_waits(q, need)
        sem = self._sem(key)
        ins = self.eng[q].dma_start(out=out, in_=in_, **kw)
        self.nops[q] += 1
        self.n_ins += 1
        self.count[key] += 16
        ins.then_inc(sem, 16)
        tok = (key, self.count[key], "dma", 0)
        rw = self.dma_res_w.setdefault(key, set())
        for r in list(rw):
            t = self.last_w.get(r)
            if t is not None and t[0] == key:
                self.last_w[r] = tok
            else:
                rw.discard(r)
        rr = self.dma_res_r.setdefault(key, set())
        for r in list(rr):
            d = self.readers.get(r, {})
            if key in d:
                d[key] = tok
            else:
                rr.discard(r)
        self._record(tok, reads, writes)
        rw.update(writes)
        rr.update(reads)
        return tok

    def barrier(self):
        need = {k: v for k, v in self.count.items() if v > 0}
        for e in self.ENGS:
            p = self.pending[e]
            for k, v in need.items():
                if p.get(k, 0) < v:
                    p[k] = v

    def final_wait(self, eng="sp"):
        need = {k: v for k, v in self.count.items() if v > 0}
        self._do_waits(eng, need)


def build_program():
    nc = bass.Bass("TRN2", target_bir_lowering=False)

    def din(name, shape):
        return nc.dram_tensor(name, list(shape), F32, kind="ExternalInput").ap()

    def dout(name, shape):
        return nc.dram_tensor(name, list(shape), F32, kind="ExternalOutput").ap()

    xp = din("xp", [SEQ, D])
    xs = din("xs", [NS_TOK, D])
    ksin = din("ksin", [SB, 128, 128])
    vsin = din("vsin", [SB, 128, 128])
    csin = din("csin", [SB, 30, 512])
    call = din("call", [1 + SB, D])
    wadaf = din("wadaf", [128, 8 * 4 * D])
    wadat = din("wadat", [D, 2 * D])
    bada = din("bada", [1, 6 * D])
    winx = din("winx", [128, 8 * NEXT])
    wao = din("wao", [64, 8 * D])
    wco = din("wco", [128, 4 * D])
    wout = din("wout", [D, D])
    wfi = din("wfi", [128, 8 * 2 * FFN])
    wfo = din("wfo", [FFN, D])
    vecs = din("vecs", [NVEC, 128])
    fng = din("fng", [1, D])
    sinks = din("sinks", [1, 8])
    ropefm = din("ropefm", [128, 2, SEQ + NS_TOK])
    ropetm = din("ropetm", [128 + NS_TOK, 2, 128])
    masks = din("masks", [128, 256 + 512 + 512])
    identd = din("identd", [128, 128])

    yp = dout("yp", [SEQ, D])
    ys = dout("ys", [NS_TOK, D])
    kwp = dout("kwp", [128, 128])
    vwp = dout("vwp", [128, 128])
    cvp = dout("cvp", [30, 512])
    kws = dout("kws", [SB, 128, 128])
    vws = dout("vws", [SB, 128, 128])
    cvs = dout("cvs", [SB, 30, 512])

    with contextlib.ExitStack() as st:
        S = Sched(nc, st)

        def T(name, shape, dt, stack=st):
            return stack.enter_context(nc.sbuf_tensor(name, list(shape), dt))

        ps = [st.enter_context(nc.psum_tensor("ps%d" % i, [128, 512], F32)) for i in range(8)]
        bank_rr = [0]

        def nb_():
            b = bank_rr[0]
            bank_rr[0] = (b + 1) % 8
            return b

        ring = T("ring", [128, NSLOT, SLOT], BF16)
        hT = T("hT", [128, 8, TT], BF16)
        qTz = [T("qTz0", [128, 4, TT], BF16), T("qTz1", [128, 4, TT], BF16)]
        aT = T("aT", [128, 8, TT], BF16)
        nT = T("nT", [128, 4, TT], BF16)
        mT = T("mT", [128, 8, TT], BF16)
        hidT = T("hidT", [128, NHC, TT], BF16)
        xn = T("xn", [128, 2, D], F32)
        sf = T("sf", [128, 6, 512], F32)
        dwb = T("dwb", [128, 4, 512], F32)
        ident = T("ident", [128, 128], F32)
        onesb = T("onesb", [128, 64], BF16)
        onesdiv = T("onesdiv", [128, 128], F32)
        vecT = T("vecT", [128, NVEC], F32)
        modsT = T("modsT", [128, 4, 8, 17], F32)
        G1p = T("G1p", [128, D], F32)
        G2p = T("G2p", [128, D], F32)
        Gf = T("Gf", [128, D], F32)
        sinkexp = T("sinkexp", [64, 8], F32)
        stt = T("stt", [128, 8, 4], F32)
        epsc = T("epsc", [128, 2], F32)

        sf_rr = [0]

        def sfn():
            i = sf_rr[0]
            sf_rr[0] = (i + 1) % 4
            return i

        st_rr = [0]

        def stn():
            i = st_rr[0]
            st_rr[0] = (i + 1) % 8
            return i

        xn_rr = [0]

        plan = []
        state = {"issued": 0, "next": 0}

        def w_view(s, shape, parts=128):
            n = 1
            for d_ in shape:
                n *= d_
            v = ring[:parts, s, 0:n]
            if len(shape) == 2:
                return v.rearrange("p (a b) -> p a b", a=shape[0])
            return v

        def issue_next():
            i = state["issued"]
            if i >= len(plan):
                return
            name, shape, parts, src = plan[i]
            s = i % NSLOT
            if len(src.shape) == 2:
                n = 1
                for d_ in shape:
                    n *= d_
                dst = ring[:parts, s, 0:n]
            else:
                dst = w_view(s, shape, parts)
            S.dma("pool", "w%d" % s, dst, src, writes=[("w", s)])
            state["issued"] += 1

        def w_get(name):
            i = state["next"]
            assert plan[i][0] == name, (plan[i][0], name)
            assert i < state["issued"], "weight piece not issued: ring too small for live set"
            state["next"] += 1
            s = i % NSLOT
            return s, w_view(s, plan[i][1], plan[i][2])

        def w_done():
            issue_next()

        wadat_v = wadat.rearrange("(kc p) n -> p kc n", p=128)
        wout_v = wout.rearrange("(kc p) n -> p kc n", p=128)
        wfo_v = wfo.rearrange("(kc p) n -> p kc n", p=128)

        def pm(ap, c0, w, nk=8):
            return ap[:, nk * c0:nk * (c0 + w)]

        for mi in range(4):
            for p in range(4):
                plan.append(("ada_fm", (8, 256), 128, pm(wadaf, mi * D + p * 256, 256)))
        for m_idx in range(2):
            for p in range(4):
                plan.append(("ada_tm", (2, 1024), 128, wadat_v[:, 2 * p:2 * p + 2, m_idx * D:(m_idx + 1) * D]))

        def plan_p2():
            for j in range(4):
                plan.append(("qp", (8, 256), 128, pm(winx, C_QP[j], 256)))
            plan.append(("kp", (8, 256), 128, pm(winx, C_KP, 256)))
            plan.append(("v", (8, 128), 128, pm(winx, C_V, 128)))
            for c in range(4):
                plan.append(("glu", (8, 256), 128, pm(winx, C_GLU[c], 256)))

        def plan_merge():
            for p in range(4):
                plan.append(("gatt", (8, 256), 128, pm(winx, C_GATT[p], 256)))
                plan.append(("gconv", (8, 256), 128, pm(winx, C_GCONV[p], 256)))
                plan.append(("ao", (8, 256), 64, pm(wao, p * 256, 256)))
                plan.append(("co", (4, 256), 128, pm(wco, p * 256, 256, nk=4)))
            for hf in range(2):
                for p in range(4):
                    plan.append(("wo", (2, 512), 128, wout_v[:, 2 * p:2 * p + 2, hf * 512:(hf + 1) * 512]))

        def plan_ffn():
            for hp in range(NHC // 2):
                plan.append(("fg", (8, 256), 128, pm(wfi, hp * 256, 256)))
                plan.append(("fu", (8, 256), 128, pm(wfi, FFN + hp * 256, 256)))
            for hf in range(2):
                for p in range(NHC // 2):
                    plan.append(("fo", (2, 512), 128, wfo_v[:, 2 * p:2 * p + 2, hf * 512:(hf + 1) * 512]))

        plan_p2(); plan_merge(); plan_ffn()
        plan_p2()
        for ti_ in range(NTILE):
            plan_merge()
            if ti_ + 1 < NTILE:
                plan_p2()
            plan_ffn()

        S.dma("sp", "cst", ident[:], identd, writes=["ident"])
        msk32 = sf[:, 0:3, :].rearrange("p a b -> p (a b)")[:, 0:1280]
        S.dma("sp", "cst", msk32, masks, writes=[("sf", 0), ("sf", 1), ("sf", 2)])
        S.dma("sp", "cst", Gf[:], fng.partition_broadcast(128), writes=["Gf"])
        S.dma("sp", "cst", sinkexp[:], sinks.partition_broadcast(64), writes=["sinkexp"])
        for _ in range(NSLOT):
            issue_next()
        S.op("dve", lambda: nc.vector.memset(onesb[:], 1.0), writes=["onesb"])
        S.op("dve", lambda: nc.vector.memset(qTz[0][:], 0.0), writes=["qT"])
        S.op("dve", lambda: nc.vector.memset(qTz[1][:], 0.0), writes=["qT"])
        S.op("dve", lambda: nc.vector.memset(aT[:], 0.0), writes=["aT"])
        S.op("dve", lambda: nc.vector.memset(onesdiv[:], 1.0 / 512), writes=["onesdiv"])
        S.op("dve", lambda: nc.vector.memset(epsc[:, 0:1], RMS_EPS), writes=["epsc"])
        S.op("dve", lambda: nc.vector.memset(epsc[:, 1:2], LN_EPS), writes=["epsc"])
        S.op("act", lambda: nc.scalar.activation(out=sinkexp[:], in_=sinkexp[:], func=AF.Exp), reads=["sinkexp"], writes=["sinkexp"])

        vrows = sf[:, 3, 0:128]
        vrows2 = sf[:, 4, 0:128]
        S.dma("sp", "cst", vrows, vecs[0:128, :], writes=[("sf", 3)])
        S.dma("sp", "cst", vrows2[0:NVEC - 128, :], vecs[128:NVEC, :], writes=[("sf", 4)])
        b = nb_()
        S.group("pe", [lambda: nc.tensor.transpose(ps[b][:, 0:128], vrows, ident[:]),
                       lambda: nc.tensor.transpose(ps[b][:, 128:NVEC], vrows2[0:NVEC - 128, :], ident[0:NVEC - 128, 0:NVEC - 128])],
                reads=[("sf", 3), ("sf", 4), "ident"], writes=[("ps", b)])
        S.op("dve", lambda: nc.vector.tensor_copy(out=vecT[:], in_=ps[b][:, 0:NVEC]), writes=[("ps", b), "vecT"])

        with contextlib.ExitStack() as stA:
            mk = T("mk", [128, 256 + 512 + 512], BF16, stA)
            S.op("dve", lambda: nc.vector.tensor_copy(out=mk[:], in_=msk32), reads=[("sf", 0), ("sf", 1), ("sf", 2)], writes=["mk"])
            maskSb = mk[:, 256:768]
            maskSn = mk[0:64, 768:1280]
            cs_t = T("cs_t", [1 + SB, D], F32, stA)
            scT = T("scT", [128, 8, 1 + SB], BF16, stA)
            scTp = T("scTp", [128, 8, 128], BF16, stA)
            scTs = T("scTs", [128, 8, NS_TOK], BF16, stA)
            G1s = T("G1s", [NS_TOK, D], F32, stA)
            G2s = T("G2s", [NS_TOK, D], F32, stA)
            xs_t = T("xs_t", [NS_TOK, 1, D], F32, stA)
            ksb = T("ksb", [128, SB, 128], F32, stA)
            vsb = T("vsb", [128, SB, 128], F32, stA)
            kbT = T("kbT", [128, SB, 128], BF16, stA)
            vsb16 = T("vsb16", [128, SB, 128], BF16, stA)
            csb = T("csb", [120, 4, 512], F32, stA)
            uh = T("uh", [128, 4, SB, 34], F32, stA)
            PTb = T("PTb", [128, 512], BF16, stA)
            PTn = T("PTn", [64, 512], BF16, stA)
            kTs = T("kTs", [128, NS_TOK], BF16, stA)
            vnew16 = T("vnew16", [NS_TOK, 128], BF16, stA)
            tabS = T("tabS", [128, 2, NS_TOK], F32, stA)
            tabtS = T("tabtS", [NS_TOK, 2, 128], F32, stA)
            otok = T("otok", [NS_TOK, 128 + 128 + 512], F32, stA)
            qs2z = [T("qs2z0", [128, SB, 4, ST], BF16, stA), T("qs2z1", [128, SB, 4, ST], BF16, stA)]

            S.dma("sp", "cst", cs_t[:], call, writes=["cs_t"])
            S.dma("sp", "xs", xs_t[:, 0, :], xs, writes=[("x", 0)])
            S.dma("sp", "cst", ksb[:], ksin.rearrange("b k f -> k b f"), writes=["ksb"])
            S.dma("sp", "cst", vsb[:], vsin.rearrange("b k f -> k b f"), writes=["vsb"])
            S.dma("sp", "cst", csb[:], csin.rearrange("(rb bl) s f -> (bl s) rb f", bl=4), writes=["csb"])
            S.dma("sp", "cst", tabS[:], ropefm[:, :, SEQ:SEQ + NS_TOK], writes=["tab"])
            S.dma("sp", "cst", tabtS[:], ropetm[128:128 + NS_TOK, :, :], writes=["tabt"])
            S.dma("sp", "o", kws[:, 0:124, :], ksin[:, 4:128, :])
            S.dma("sp", "o", vws[:, 0:124, :], vsin[:, 4:128, :])
            S.dma("sp", "o", cvs[:, 0:26, :], csin[:, 4:30, :])

            S.op("act", lambda: nc.scalar.activation(out=cs_t[:], in_=cs_t[:], func=AF.Silu), reads=["cs_t"], writes=["cs_t"])
            for half in range(2):
                b = nb_()
                S.group("pe", [lambda c=c: nc.tensor.transpose(ps[b][:, (c % 4) * 32:(c % 4) * 32 + 17],
                                                               cs_t[:, c * 128:(c + 1) * 128], ident[0:17, 0:17])
                               for c in range(half * 4, half * 4 + 4)],
                        reads=["cs_t", "ident"], writes=[("ps", b)])
                S.op("dve", lambda half=half, b=b: nc.vector.tensor_copy(
                    out=scT[:, half * 4:half * 4 + 4, :],
                    in_=ps[b][:, 0:128].rearrange("p (a c) -> p a c", a=4)[:, :, 0:17]),
                    writes=[("ps", b), "scT"])
            S.op("dve", lambda: nc.vector.tensor_copy(out=scTp[:], in_=scT[:, :, 0:1].broadcast_to([128, 8, 128])),
                 reads=["scT"], writes=["scTp"])
            for kc in range(8):
                S.op("dve", lambda kc=kc: nc.vector.tensor_copy(
                    out=scTs[:, kc, :].rearrange("p (b t) -> p b t", t=ST),
                    in_=scT[:, kc, 1:1 + SB].unsqueeze(2).broadcast_to([128, SB, ST])),
                    reads=["scT"], writes=["scTs"])

            vbase = [V_BSH1, V_BSC1, V_BSH2, V_BSC2]
            for mi in range(4):
                for p in range(4):
                    s, wv = w_get("ada_fm")
                    for cl in range(2):
                        c = 2 * p + cl
                        b = nb_()
                        S.group("pe", [lambda kc=kc, cl=cl, b=b: nc.tensor.matmul(
                            ps[b][:, 0:17], wv[:, kc, cl * 128:(cl + 1) * 128], scT[:, kc, :], start=(kc == 0), stop=(kc == 7))
                            for kc in range(8)], reads=[("w", s), "scT"], writes=[("ps", b)])
                        S.op("dve", lambda mi=mi, c=c, b=b: nc.vector.tensor_scalar(
                            out=modsT[:, mi, c, :], in0=ps[b][:, 0:17], scalar1=vecT[:, vbase[mi] + c:vbase[mi] + c + 1],
                            scalar2=None, op0=ALU.add), reads=["vecT"], writes=[("ps", b), "modsT"])
                    w_done()
            for mi, vg in ((1, V_N1G), (3, V_N2G)):
                for c in range(8):
                    S.op("dve", lambda mi=mi, vg=vg, c=c: nc.vector.tensor_scalar(
                        out=modsT[:, mi, c, :], in0=modsT[:, mi, c, :], scalar1=1.0, scalar2=vecT[:, vg + c:vg + c + 1],
                        op0=ALU.add, op1=ALU.mult), reads=["modsT", "vecT"], writes=["modsT"])
            for m, Gp, Gs in ((2, G1p, G1s), (5, G2p, G2s)):
                bb = [nb_() for _ in range(4)]
                bbc = sf[:, 4:6, :].rearrange("p a b -> p (a b)")
                S.dma("sp", "cst", bbc, bada[:, m * D:(m + 1) * D].partition_broadcast(128), writes=[("sf", 4), ("sf", 5)])
                for p in range(4):
                    s, wv = w_get("ada_tm")
                    fns = []
                    for kl in range(2):
                        kc = 2 * p + kl
                        for hf in range(2):
                            fns.append(lambda kc=kc, kl=kl, hf=hf: nc.tensor.matmul(
                                ps[bb[hf]][:, :], scTp[:, kc, :], wv[:, kl, hf * 512:(hf + 1) * 512], start=(kc == 0), stop=(kc == 7)))
                            fns.append(lambda kc=kc, kl=kl, hf=hf: nc.tensor.matmul(
                                ps[bb[2 + hf]][0:NS_TOK, :], scTs[:, kc, :], wv[:, kl, hf * 512:(hf + 1) * 512], start=(kc == 0), stop=(kc == 7)))
                    S.group("pe", fns, reads=[("w", s), "scTp", "scTs"], writes=[("ps", x) for x in bb])
                    w_done()
                for hf in range(2):
                    S.op("dve", lambda hf=hf, Gp=Gp: nc.vector.tensor_tensor(
                        out=Gp[:, hf * 512:(hf + 1) * 512], in0=ps[bb[hf]][:, :], in1=bbc[:, hf * 512:(hf + 1) * 512], op=ALU.add),
                        reads=[("sf", 4), ("sf", 5)], writes=[("ps", bb[hf]), "G"])
                    S.op("dve", lambda hf=hf, Gs=Gs: nc.vector.tensor_tensor(
                        out=Gs[:, hf * 512:(hf + 1) * 512], in0=ps[bb[2 + hf]][0:NS_TOK, :], in1=bbc[0:NS_TOK, hf * 512:(hf + 1) * 512], op=ALU.add),
                        reads=[("sf", 4), ("sf", 5)], writes=[("ps", bb[2 + hf]), "G"])

            if KSTOP <= 1:
                S.final_wait("sp")
                return nc
            cur = {"h": hT, "hk": "hT"}

            def norm_gen(kind, xsrc, nb, bp, mi_a, mi_b, xkey="x", hdst=None, hkey="hT"):
                hd = hT if hdst is None else hdst
                info = {}

                def stA(b):
                    xb = xsrc(b)
                    xi = xn_rr[0]
                    xn_rr[0] ^= 1
                    k = stn()
                    info[b] = xi
                    S.op("act", lambda: nc.scalar.memzero(stt[:, k, :]), writes=[("st", k)])
                    S.op("act", lambda: nc.scalar.activation(out=xn[:bp, xi, :], in_=xb, func=AF.Square, accum_out=stt[:bp, k, 0:1]),
                         reads=[(xkey, b)], writes=[("xn", xi), ("st", k)])
                    S.op("act", lambda: nc.scalar.activation(out=stt[:bp, k, 1:2], in_=stt[:bp, k, 0:1], func=AF.Ln, scale=1.0 / D, bias=epsc[:bp, 0:1]),
                         reads=[("st", k), "epsc"], writes=[("st", k)])
                    S.op("act", lambda: nc.scalar.activation(out=stt[:bp, k, 2:3], in_=stt[:bp, k, 1:2], func=AF.Exp, scale=-0.5),
                         reads=[("st", k)], writes=[("st", k)])
                    S.op("dve", lambda: nc.vector.tensor_scalar(out=xn[:bp, xi, :], in0=xb, scalar1=stt[:bp, k, 2:3], scalar2=None, op0=ALU.mult),
                         reads=[(xkey, b), ("st", k)], writes=[("xn", xi)])

                def stB(b):
                    xi = info[b]
                    pbs = []
                    for half in range(2):
                        pb = nb_()
                        pbs.append(pb)
                        S.group("pe", [lambda c=c: nc.tensor.transpose(
                            ps[pb][:, (c % 4) * 128:(c % 4) * 128 + bp], xn[:bp, xi, c * 128:(c + 1) * 128], ident[:bp, :bp])
                            for c in range(half * 4, half * 4 + 4)], reads=[("xn", xi), "ident"], writes=[("ps", pb)])
                    for half in range(2):
                        pb = pbs[half]
                        for c in range(half * 4, half * 4 + 4):
                            src = ps[pb][:, (c % 4) * 128:(c % 4) * 128 + bp]
                            dst = hd[:, c, b * bp:(b + 1) * bp]
                            if kind == "p":
                                if half == 0:
                                    S.op("act", lambda: nc.scalar.activation(
                                        out=dst, in_=src, func=AF.Identity, scale=modsT[:, mi_a, c, 0:1], bias=modsT[:, mi_b, c, 0:1]),
                                        reads=["modsT"], writes=[("ps", pb), (hkey, 0)])
                                else:
                                    S.op("dve", lambda: nc.vector.tensor_scalar(
                                        out=dst, in0=src, scalar1=modsT[:, mi_a, c, 0:1], scalar2=modsT[:, mi_b, c, 0:1], op0=ALU.mult, op1=ALU.add),
                                        reads=["modsT"], writes=[("ps", pb), (hkey, 1)])
                            else:
                                i = sfn()
                                tmp = sf[:, i, 0:NS_TOK]
                                S.op("dve", lambda: nc.vector.tensor_tensor(
                                    out=tmp.rearrange("p (b t) -> p b t", t=ST), in0=src.rearrange("p (b t) -> p b t", t=ST),
                                    in1=modsT[:, mi_a, c, 1:1 + SB].unsqueeze(2).broadcast_to([128, SB, ST]), op=ALU.mult),
                                    reads=["modsT"], writes=[("ps", pb), ("sf", i)])
                                S.op("dve", lambda: nc.vector.tensor_tensor(
                                    out=dst.rearrange("p (b t) -> p b t", t=ST), in0=tmp.rearrange("p (b t) -> p b t", t=ST),
                                    in1=modsT[:, mi_b, c, 1:1 + SB].unsqueeze(2).broadcast_to([128, SB, ST]), op=ALU.add),
                                    reads=["modsT", ("sf", i)], writes=[(hkey, half)])

                stA(0)
                yield
                for b in range(nb):
                    if b + 1 < nb:
                        stA(b + 1)
                        yield
                    stB(b)
                    yield

            def norm_to_hT(*args, **kw):
                for _ in norm_gen(*args, **kw):
                    pass

            def fm_mm(bank, wv, s, col0, src, nk, nt, kp=128, extra_reads=()):
                S.group("pe", [lambda k=k: nc.tensor.matmul(ps[bank][:, 0:nt], wv[:kp, k, col0:col0 + 128], src[:kp, k, 0:nt],
                                                           start=(k == 0), stop=(k == nk - 1)) for k in range(nk)],
                        reads=[("w", s)] + list(extra_reads), writes=[("ps", bank)])

            def tm_mm(bank, c0, wv, s, kidx, col0, ncol, tsl, M, start, stop, extra_reads=()):
                S.group("pe", [lambda k=k: nc.tensor.matmul(ps[bank][:M, c0:c0 + ncol], cur["h"][:, k, tsl], wv[:, k, col0:col0 + ncol],
                                                           start=(start and k == kidx[0]), stop=(stop and k == kidx[-1])) for k in kidx],
                        reads=[("w", s), (cur["hk"], 0), (cur["hk"], 1)] + list(extra_reads), writes=[("ps", bank)])

            def phase2(kind, nt, tabC, tabS_, kT_dst, v_evac, u_dst, tok_extra, side=None):
                for j in range(5):
                    s, wv = w_get("qp" if j < 4 else "kp")
                    b0, b1 = nb_(), nb_()
                    fm_mm(b0, wv, s, 0, cur["h"], 8, nt, extra_reads=[(cur["hk"], 0), (cur["hk"], 1)])
                    fm_mm(b1, wv, s, 128, cur["h"], 8, nt, extra_reads=[(cur["hk"], 0), (cur["hk"], 1)])
                    i0, i1 = sfn(), sfn()
                    S.op("dve", lambda i0=i0, b0=b0: nc.vector.tensor_tensor(out=sf[:, i0, 0:nt], in0=ps[b0][:, 0:nt], in1=tabC, op=ALU.mult),
                         reads=["tab"], writes=[("ps", b0), ("sf", i0)])
                    S.op("dve", lambda i1=i1, b1=b1: nc.vector.tensor_tensor(out=sf[:, i1, 0:nt], in0=ps[b1][:, 0:nt], in1=tabS_, op=ALU.mult),
                         reads=["tab"], writes=[("ps", b1), ("sf", i1)])
                    if j < 4:
                        for g in range(2):
                            gs_ = slice(g * 64, (g + 1) * 64)
                            S.op("dve", lambda i0=i0, i1=i1, g=g, gs_=gs_: nc.vector.tensor_tensor(
                                out=qTz[g][gs_, j, 0:nt], in0=sf[gs_, i0, 0:nt], in1=sf[gs_, i1, 0:nt], op=ALU.add),
                                reads=[("sf", i0), ("sf", i1)], writes=["qT"])
                    else:
                        S.op("dve", lambda i0=i0, i1=i1: nc.vector.tensor_tensor(out=kT_dst, in0=sf[:, i0, 0:nt], in1=sf[:, i1, 0:nt], op=ALU.add),
                             reads=[("sf", i0), ("sf", i1)], writes=["kTcur"])
                    if j == 4 and tok_extra is not None:
                        tsl, M, tc_, ts_, kdst = tok_extra["tsl"], tok_extra["M"], tok_extra["tabtC"], tok_extra["tabtS"], tok_extra["k"]
                        bk = nb_()
                        tm_mm(bk, 0, wv, s, list(range(8)), 0, 256, tsl, M, True, True)
                        i0, i1 = sfn(), sfn()
                        S.op("dve", lambda: nc.vector.tensor_tensor(out=sf[:M, i0, 0:128], in0=ps[bk][:M, 0:128], in1=tc_, op=ALU.mult),
                             reads=["tabt"], writes=[("ps", bk), ("sf", i0)])
                        S.op("dve", lambda: nc.vector.tensor_tensor(out=sf[:M, i1, 0:128], in0=ps[bk][:M, 128:256], in1=ts_, op=ALU.mult),
                             reads=["tabt"], writes=[("ps", bk), ("sf", i1)])
                        S.op("dve", lambda: nc.vector.tensor_tensor(out=kdst, in0=sf[:M, i0, 0:128], in1=sf[:M, i1, 0:128], op=ALU.add),
                             reads=[("sf", i0), ("sf", i1)], writes=[tok_extra["okey"]])
                    w_done()
                    if side is not None:
                        next(side, None)
                s, wv = w_get("v")
                v_evac(s, wv)
                w_done()
                if side is not None:
                    next(side, None)
                for c in range(4):
                    s, wv = w_get("glu")
                    b0, b1 = nb_(), nb_()
                    fm_mm(b0, wv, s, 0, cur["h"], 8, nt, extra_reads=[(cur["hk"], 0), (cur["hk"], 1)])
                    fm_mm(b1, wv, s, 128, cur["h"], 8, nt, extra_reads=[(cur["hk"], 0), (cur["hk"], 1)])
                    i1 = sfn()
                    S.op("act", lambda i1=i1, b1=b1: nc.scalar.activation(out=sf[:, i1, 0:nt], in_=ps[b1][:, 0:nt], func=AF.Sigmoid),
                         writes=[("ps", b1), ("sf", i1)])
                    u_dst(c, b0, i1)
                    if tok_extra is not None:
                        tsl, M, udst = tok_extra["tsl"], tok_extra["M"], tok_extra["u"]
                        bk = nb_()
                        tm_mm(bk, 0, wv, s, list(range(8)), 0, 256, tsl, M, True, True)
                        i2 = sfn()
                        S.op("act", lambda: nc.scalar.activation(out=sf[:M, i2, 0:128], in_=ps[bk][:M, 128:256], func=AF.Sigmoid),
                             writes=[("ps", bk), ("sf", i2)])
                        S.op("dve", lambda c=c: nc.vector.tensor_tensor(out=udst[:, c * 128:(c + 1) * 128], in0=ps[bk][:M, 0:128], in1=sf[:M, i2, 0:128], op=ALU.mult),
                             reads=[("sf", i2)], writes=[("ps", bk), tok_extra["okey"]])
                    w_done()
                    if side is not None:
                        next(side, None)
                if side is not None:
                    for _ in side:
                        pass

            def conv_taps(nt, usrc, accv):
                for j in range(CW):
                    for c in range(4):
                        wcol = vecT[:, V_CWT + j * 4 + c:V_CWT + j * 4 + c + 1]
                        if j == 0:
                            S.op("dve", lambda c=c, wcol=wcol: nc.vector.tensor_scalar(
                                out=accv(c), in0=usrc(c, 0), scalar1=wcol, scalar2=vecT[:, V_CB + c:V_CB + c + 1], op0=ALU.mult, op1=ALU.add),
                                reads=["u", "vecT"], writes=[("dwb", c)])
                        else:
                            S.op("dve", lambda c=c, j=j, wcol=wcol: nc.vector.scalar_tensor_tensor(
                                out=accv(c), in0=usrc(c, j), scalar=wcol, in1=accv(c), op0=ALU.mult, op1=ALU.add),
                                reads=["u", "vecT", ("dwb", c)], writes=[("dwb", c)])
                    yield j

            def ln_gen(nt):
                bm, be = nb_(), nb_()
                S.group("pe", [lambda c=c: nc.tensor.matmul(ps[bm][:, 0:nt], onesdiv[:], dwb[:, c, 0:nt], start=(c == 0), stop=(c == 3)) for c in range(4)],
                        reads=["onesdiv"] + [("dwb", c) for c in range(4)], writes=[("ps", bm)])
                sq = []
                for c in range(4):
                    i = sfn()
                    sq.append(i)
                    S.op("act", lambda: nc.scalar.activation(out=sf[:, i, 0:nt], in_=dwb[:, c, 0:nt], func=AF.Square),
                         reads=[("dwb", c)], writes=[("sf", i)])
                S.group("pe", [lambda c=c: nc.tensor.matmul(ps[be][:, 0:nt], onesdiv[:], sf[:, sq[c], 0:nt], start=(c == 0), stop=(c == 3)) for c in range(4)],
                        reads=["onesdiv"] + [("sf", i) for i in sq], writes=[("ps", be)])
                im, ir = 4, 5
                i2 = sfn()
                S.op("act", lambda: nc.scalar.copy(out=sf[:, im, 0:nt], in_=ps[bm][:, 0:nt]), writes=[("ps", bm), ("sf", im)])
                S.op("act", lambda: nc.scalar.activation(out=sf[:, i2, 0:nt], in_=ps[bm][:, 0:nt], func=AF.Square), writes=[("ps", bm), ("sf", i2)])
                S.op("dve", lambda: nc.vector.tensor_tensor(out=sf[:, ir, 0:nt], in0=ps[be][:, 0:nt], in1=sf[:, i2, 0:nt], op=ALU.subtract),
                     reads=[("sf", i2)], writes=[("ps", be), ("sf", ir)])
                S.op("act", lambda: nc.scalar.activation(out=sf[:, ir, 0:nt], in_=sf[:, ir, 0:nt], func=AF.Ln, bias=epsc[:, 1:2]),
                     reads=[("sf", ir), "epsc"], writes=[("sf", ir)])
                S.op("act", lambda: nc.scalar.activation(out=sf[:, ir, 0:nt], in_=sf[:, ir, 0:nt], func=AF.Exp, scale=-0.5),
                     reads=[("sf", ir)], writes=[("sf", ir)])
                yield
                for c in range(4):
                    S.op("dve", lambda: nc.vector.tensor_tensor(out=dwb[:, c, 0:nt], in0=dwb[:, c, 0:nt], in1=sf[:, im, 0:nt], op=ALU.subtract),
                         reads=[("dwb", c), ("sf", im)], writes=[("dwb", c)])
                    S.op("dve", lambda: nc.vector.tensor_tensor(out=dwb[:, c, 0:nt], in0=dwb[:, c, 0:nt], in1=sf[:, ir, 0:nt], op=ALU.mult),
                         reads=[("dwb", c), ("sf", ir)], writes=[("dwb", c)])
                    S.op("act", lambda: nc.scalar.activation(out=nT[:, c, 0:nt], in_=dwb[:, c, 0:nt], func=AF.Silu,
                                                             scale=vecT[:, V_LG + c:V_LG + c + 1], bias=vecT[:, V_LB + c:V_LB + c + 1]),
                         reads=[("dwb", c), "vecT"], writes=["nT"])
                    yield

            def ln_part(nt):
                for _ in ln_gen(nt):
                    pass

            def merge(nt, side=None):
                for p in range(4):
                    sg, wg = w_get("gatt")
                    sc_, wc = w_get("gconv")
                    sa, wa = w_get("ao")
                    so, wo_ = w_get("co")
                    for cl in range(2):
                        dc = 2 * p + cl
                        ba, bc, bg, bh = nb_(), nb_(), nb_(), nb_()
                        fm_mm(ba, ring[:, sa, 0:2048].rearrange("p (a b) -> p a b", a=8), sa, cl * 128, aT, 8, nt, extra_reads=["aT"])
                        fm_mm(bc, wo_, so, cl * 128, nT, 4, nt, extra_reads=["nT"])
                        fm_mm(bg, wg, sg, cl * 128, cur["h"], 8, nt, extra_reads=[(cur["hk"], 0), (cur["hk"], 1)])
                        fm_mm(bh, wc, sc_, cl * 128, cur["h"], 8, nt, extra_reads=[(cur["hk"], 0), (cur["hk"], 1)])
                        ig, ih, i1, i2 = sfn(), sfn(), sfn(), sfn()
                        S.op("act", lambda: nc.scalar.activation(out=sf[:, ig, 0:nt], in_=ps[bg][:, 0:nt], func=AF.Sigmoid), writes=[("ps", bg), ("sf", ig)])
                        S.op("act", lambda: nc.scalar.activation(out=sf[:, ih, 0:nt], in_=ps[bh][:, 0:nt], func=AF.Sigmoid), writes=[("ps", bh), ("sf", ih)])
                        S.op("dve", lambda: nc.vector.tensor_tensor(out=sf[:, i1, 0:nt], in0=ps[ba][:, 0:nt], in1=sf[:, ig, 0:nt], op=ALU.mult),
                             reads=[("sf", ig)], writes=[("ps", ba), ("sf", i1)])
                        S.op("dve", lambda: nc.vector.tensor_tensor(out=sf[:, i2, 0:nt], in0=ps[bc][:, 0:nt], in1=sf[:, ih, 0:nt], op=ALU.mult),
                             reads=[("sf", ih)], writes=[("ps", bc), ("sf", i2)])
                        S.op("dve", lambda dc=dc: nc.vector.tensor_tensor(out=mT[:, dc, 0:nt], in0=sf[:, i1, 0:nt], in1=sf[:, i2, 0:nt], op=ALU.add),
                             reads=[("sf", i1), ("sf", i2)], writes=["mT"])
                        if side is not None:
                            next(side, None)
                    for _ in range(4):
                        w_done()
                if side is not None:
                    for _ in side:
                        pass

            def tm_proj_resid(nb, bp, src, srckey, wname, npiece, xdst, G, xkey="x", side=None):
                nk = 2 * npiece
                for hf in range(2):
                    banks = [hf * nb + b for b in range(nb)]
                    bank_rr[0] = 4 if hf == 0 else 0
                    for p in range(npiece):
                        s, wv = w_get(wname)
                        fns = []
                        for kl in range(2):
                            kc = 2 * p + kl
                            for b in range(nb):
                                fns.append(lambda kc=kc, kl=kl, b=b: nc.tensor.matmul(
                                    ps[banks[b]][:bp, :], src[:, kc, b * bp:(b + 1) * bp], wv[:, kl, :],
                                    start=(kc == 0), stop=(kc == nk - 1)))
                        S.group("pe", fns, reads=[("w", s), srckey], writes=[("ps", x) for x in banks])
                        w_done()
                        if side is not None:
                            next(side, None)
                    for b in range(nb):
                        i = sfn()
                        S.op("dve", lambda b=b, i=i: nc.vector.tensor_tensor(
                            out=sf[:bp, i, :], in0=ps[banks[b]][:bp, :], in1=G[:bp, hf * 512:(hf + 1) * 512], op=ALU.mult),
                            reads=["G"], writes=[("ps", banks[b]), ("sf", i)])
                        S.op("pool", lambda b=b, i=i: nc.gpsimd.tensor_tensor(
                            out=xdst(b)[:, hf * 512:(hf + 1) * 512], in0=xdst(b)[:, hf * 512:(hf + 1) * 512], in1=sf[:bp, i, :], op=ALU.add),
                            reads=[(xkey, b), ("sf", i)], writes=[(xkey, b)])
                    bank_rr[0] = 0
                if side is not None:
                    for _ in side:
                        pass
                bank_rr[0] = 0

            def ffn_in(nt, hsrc=None, hkey="hT", side=None):
                hs = hT if hsrc is None else hsrc
                for hp in range(NHC // 2):
                    sg, wg = w_get("fg")
                    su, wu = w_get("fu")
                    for hl in range(2):
                        hc = 2 * hp + hl
                        bg, bu = nb_(), nb_()
                        fm_mm(bg, wg, sg, hl * 128, hs, 8, nt, extra_reads=[(hkey, 0), (hkey, 1)])
                        fm_mm(bu, wu, su, hl * 128, hs, 8, nt, extra_reads=[(hkey, 0), (hkey, 1)])
                        i = sfn()
                        S.op("act", lambda: nc.scalar.activation(out=sf[:, i, 0:nt], in_=ps[bg][:, 0:nt], func=AF.Silu), writes=[("ps", bg), ("sf", i)])
                        S.op("dve", lambda hc=hc: nc.vector.tensor_tensor(out=hidT[:, hc, 0:nt], in0=ps[bu][:, 0:nt], in1=sf[:, i, 0:nt], op=ALU.mult),
                             reads=[("sf", i)], writes=[("ps", bu), "hidT"])
                        if side is not None:
                            next(side, None)
                    w_done()
                    w_done()

            def final_gen(nb, bp, xdst, xkey="x", after_block=None):
                ks = {}

                def stA(b):
                    xb = xdst(b)
                    xi = xn_rr[0]
                    xn_rr[0] ^= 1
                    k = stn()
                    ks[b] = k
                    S.op("act", lambda: nc.scalar.memzero(stt[:, k, :]), writes=[("st", k)])
                    S.op("act", lambda: nc.scalar.activation(out=xn[:bp, xi, :], in_=xb, func=AF.Square, accum_out=stt[:bp, k, 0:1]),
                         reads=[(xkey, b)], writes=[("xn", xi), ("st", k)])
                    S.op("act", lambda: nc.scalar.activation(out=stt[:bp, k, 1:2], in_=stt[:bp, k, 0:1], func=AF.Ln, scale=1.0 / D, bias=epsc[:bp, 0:1]),
                         reads=[("st", k), "epsc"], writes=[("st", k)])
                    S.op("act", lambda: nc.scalar.activation(out=stt[:bp, k, 2:3], in_=stt[:bp, k, 1:2], func=AF.Exp, scale=-0.5),
                         reads=[("st", k)], writes=[("st", k)])

                def stB(b):
                    xb = xdst(b)
                    k = ks[b]
                    S.op("dve", lambda: nc.vector.scalar_tensor_tensor(out=xb, in0=xb, scalar=stt[:bp, k, 2:3], in1=Gf[:bp, :], op0=ALU.mult, op1=ALU.mult),
                         reads=[(xkey, b), ("st", k), "Gf"], writes=[(xkey, b)])
                    if after_block is not None:
                        after_block(b)

                stA(0)
                yield
                for b in range(nb):
                    if b + 1 < nb:
                        stA(b + 1)
                    stB(b)
                    yield

            def final_norm(*args, **kw):
                for _ in final_gen(*args, **kw):
                    pass

            def xs_src(b):
                return xs_t[:, 0, :]

            norm_to_hT("s", xs_src, 1, NS_TOK, 1, 0)
            if KSTOP <= 2:
                S.final_wait("sp")
                return nc

            def v_evac_s(s, wv):
                bk = nb_()
                tm_mm(bk, 0, wv, s, list(range(8)), 0, 128, slice(0, NS_TOK), NS_TOK, True, True)
                S.op("act", lambda: nc.scalar.copy(out=vnew16[:], in_=ps[bk][:NS_TOK, 0:128]), writes=[("ps", bk), "vnew16"])
                S.op("dve", lambda: nc.vector.tensor_copy(out=otok[:, 128:256], in_=ps[bk][:NS_TOK, 0:128]), writes=[("ps", bk), "otok"])

            def u_dst_s(c, b0, i1):
                S.op("dve", lambda: nc.vector.tensor_tensor(
                    out=uh[:, c, :, 30:34], in0=ps[b0][:, 0:NS_TOK].rearrange("p (b t) -> p b t", t=ST),
                    in1=sf[:, i1, 0:NS_TOK].rearrange("p (b t) -> p b t", t=ST), op=ALU.mult),
                    reads=[("sf", i1)], writes=[("ps", b0), "u"])

            phase2("s", NS_TOK, tabS[:, 0, :], tabS[:, 1, :], kTs[:], v_evac_s, u_dst_s,
                   dict(tsl=slice(0, NS_TOK), M=NS_TOK, tabtC=tabtS[:, 0, :], tabtS=tabtS[:, 1, :], k=otok[:, 0:128], u=otok[:, 256:768], okey="otok"))
            for bq in range(SB):
                S.dma("sp", "o", kws[bq, 124:128, :], otok[bq * 4:(bq + 1) * 4, 0:128], reads=["otok"])
                S.dma("sp", "o", vws[bq, 124:128, :], otok[bq * 4:(bq + 1) * 4, 128:256], reads=["otok"])
                S.dma("sp", "o", cvs[bq, 26:30, :], otok[bq * 4:(bq + 1) * 4, 256:768], reads=["otok"])

            if KSTOP <= 3:
                S.final_wait("sp")
                return nc
            for grp in range(4):
                pb = nb_()
                S.group("pe", [lambda bq=bq, pb=pb: nc.tensor.transpose(ps[pb][:, (bq % 4) * 128:(bq % 4 + 1) * 128], ksb[:, bq, :], ident[:])
                               for bq in range(grp * 4, grp * 4 + 4)], reads=["ksb", "ident"], writes=[("ps", pb)])
                S.op("act", lambda grp=grp, pb=pb: nc.scalar.copy(out=kbT[:, grp * 4:(grp + 1) * 4, :].rearrange("p a b -> p (a b)"), in_=ps[pb][:, :]),
                     writes=[("ps", pb), "kbT"])
            S.op("dve", lambda: nc.vector.tensor_copy(out=vsb16[:], in_=vsb[:]), reads=["vsb"], writes=["vsb16"])
            if KSUB <= 1:
                S.final_wait("sp")
                return nc
            bX, bY, bZ, bW = nb_(), nb_(), nb_(), nb_()
            for g in range(2):
                S.op("dve", lambda g=g: nc.vector.memset(qs2z[g][:], 0.0), writes=["qs2"])
                for j in range(4):
                    S.op("dve", lambda j=j, g=g: nc.vector.tensor_copy(
                        out=qs2z[g][g * 64:(g + 1) * 64, :, j, :], in_=qTz[g][g * 64:(g + 1) * 64, j, 0:NS_TOK].rearrange("p (b t) -> p b t", t=ST)),
                        reads=["qT"], writes=["qs2"])
            fns = []
            for bq in range(SB):
                for g in range(2):
                    fns.append(lambda bq=bq, g=g: nc.tensor.matmul(
                        ps[bX][:, (bq * 2 + g) * 16:(bq * 2 + g) * 16 + 16],
                        kbT[:, bq, :], qs2z[g][:, bq, :, :].rearrange("p j t -> p (j t)"), start=True, stop=True))
            S.group("pe", fns, reads=["kbT", "qs2"], writes=[("ps", bX)])
            S.group("pe", [lambda g=g: nc.tensor.matmul(
                ps[bY][0:NS_TOK, g * 256:(g + 1) * 256],
                kTs[:, :], qs2z[g][:, :, :, :].rearrange("p b j t -> p (b j t)"),
                start=True, stop=True) for g in range(2)],
                reads=["kTcur", "qs2"], writes=[("ps", bY)])
            S.op("act", lambda: nc.scalar.activation(out=PTb[:], in_=ps[bX][:, :], func=AF.Exp, scale=SCALE), writes=[("ps", bX), "PTb"])
            S.op("act", lambda: nc.scalar.activation(out=PTn[:], in_=ps[bY][0:NS_TOK, :], func=AF.Exp, scale=SCALE), writes=[("ps", bY), "PTn"])
            S.op("dve", lambda: nc.vector.tensor_tensor(out=PTb[:], in0=PTb[:], in1=maskSb, op=ALU.mult), reads=["PTb", "mk"], writes=["PTb"])
            S.op("dve", lambda: nc.vector.tensor_tensor(out=PTn[:], in0=PTn[:], in1=maskSn, op=ALU.mult), reads=["PTn", "mk"], writes=["PTn"])
            if KSUB <= 2:
                S.final_wait("sp")
                return nc
            for bank, use_v in ((bZ, True), (bW, False)):
                fns = []
                for g in range(2):
                    lnew = vnew16[:, g * 64:(g + 1) * 64] if use_v else onesb[0:NS_TOK, :]
                    fns.append(lambda g=g, lnew=lnew, bank=bank: nc.tensor.matmul(
                        ps[bank][0:64, g * 256:(g + 1) * 256], lnew, PTn[:, g * 256:(g + 1) * 256], start=True, stop=True))
                    for bq in range(SB):
                        lb = vsb16[:, bq, g * 64:(g + 1) * 64] if use_v else onesb[:, :]
                        fns.append(lambda g=g, bq=bq, lb=lb, bank=bank: nc.tensor.matmul(
                            ps[bank][0:64, g * 256 + bq * 16:g * 256 + bq * 16 + 16],
                            lb, PTb[:, (bq * 2 + g) * 16:(bq * 2 + g) * 16 + 16],
                            start=False, stop=(bq == SB - 1), skip_group_check=True))
                S.group("pe", fns, reads=["PTb", "PTn", "vnew16", "vsb16", "onesb"], writes=[("ps", bank)])
            if KSUB <= 3:
                S.final_wait("sp")
                return nc
            ird = sfn()
            vj = lambda ap, j: ap.rearrange("p (b j t) -> p b j t", j=4, t=ST)[:, :, j, :]
            for g in range(2):
                for j in range(4):
                    h = g * 4 + j
                    S.op("dve", lambda g=g, j=j, h=h: nc.vector.tensor_scalar(
                        out=vj(sf[0:64, ird, g * 256:(g + 1) * 256], j), in0=vj(ps[bW][0:64, g * 256:(g + 1) * 256], j),
                        scalar1=sinkexp[:, h:h + 1], scalar2=None, op0=ALU.add),
                        reads=["sinkexp"], writes=[("ps", bW), ("sf", ird)])
            S.op("dve", lambda: nc.vector.reciprocal(out=sf[0:64, ird, :], in_=sf[0:64, ird, :]), reads=[("sf", ird)], writes=[("sf", ird)])
            for g in range(2):
                for j in range(4):
                    h = g * 4 + j
                    S.op("dve", lambda g=g, j=j, h=h: nc.vector.tensor_tensor(
                        out=aT[0:64, h, 0:NS_TOK].rearrange("p (b t) -> p b t", t=ST),
                        in0=vj(ps[bZ][0:64, g * 256:(g + 1) * 256], j), in1=vj(sf[0:64, ird, g * 256:(g + 1) * 256], j), op=ALU.mult),
                        reads=[("sf", ird)], writes=[("ps", bZ), "aT"])

            if KSTOP <= 4:
                S.final_wait("sp")
                return nc
            for c in range(4):
                pb = nb_()
                S.group("pe", [lambda rb=rb, c=c, pb=pb: nc.tensor.transpose(ps[pb][:, rb * 120:(rb + 1) * 120], csb[:, rb, c * 128:(c + 1) * 128], ident[0:120, 0:120])
                               for rb in range(4)], reads=["csb", "ident"], writes=[("ps", pb)])
                S.op("act", lambda c=c, pb=pb: nc.scalar.copy(out=uh[:, c, :, 0:30], in_=ps[pb][:, 0:480].rearrange("p (b s) -> p b s", s=30)),
                     writes=[("ps", pb), "u"])
            for _ in conv_taps(NS_TOK, lambda c, j: uh[:, c, :, j:j + ST], lambda c: dwb[:, c, 0:NS_TOK].rearrange("p (b t) -> p b t", t=ST)):
                pass
            ln_part(NS_TOK)
            merge(NS_TOK)
            tm_proj_resid(1, NS_TOK, mT, "mT", "wo", 4, xs_src, G1s)
            norm_to_hT("s", xs_src, 1, NS_TOK, 3, 2)
            ffn_in(NS_TOK)
            tm_proj_resid(1, NS_TOK, hidT, "hidT", "fo", NHC // 2, xs_src, G2s)
            final_norm(1, NS_TOK, xs_src)
            S.dma("sp", "o", ys, xs_t[:, 0, :], reads=[("x", 0)])

        if KSTOP <= 5:
            S.final_wait("sp")
            return nc
        S.barrier()

        with contextlib.ExitStack() as stB:
            xbuf = T("xbuf", [128, 2, 4, D], F32, stB)
            hB = T("hB", [128, 8, TT], BF16, stB)
            hT2 = T("hT2", [128, 8, TT], BF16, stB)
            HA = [(hT, "hT"), (hT2, "hT2")]
            print("sbuf bytes remaining (stage B):", nc.sbuf_bytes_remaining)
            kT = T("kT", [128, 128 + TT], BF16, stB)
            vtok = T("vtok", [128, 5, 128], BF16, stB)
            uT = T("uT", [128, 4, 32 + TT], F32, stB)
            tab = T("tab", [128, 2, TT], F32, stB)
            PT = T("PT", [128, 2, 2, 512], BF16, stB)
            tabtP = T("tabtP", [128, 2, 128], F32, stB)
            sinkrow = T("sinkrow", [128, 8, 128], BF16, stB)
            otokp = hidT[:, 0:3, :].rearrange("p a b -> p (a b)").bitcast(F32)
            assert tuple(otokp.shape) == (128, 768), otokp.shape
            OKEY = "hidT"

            mbias = T("mbias", [128, 2, 4, 128], BF16, stB)
            identb = T("identb", [128, 128], BF16, stB)
            S.dma("sp", "cst", sf[:, 0, 0:256], masks[:, 0:256], writes=[("sf", 0)])
            for hh in range(2):
                S.op("dve", lambda hh=hh: nc.vector.tensor_scalar(
                    out=mbias[:, hh, :, :], in0=sf[:, 0, hh * 128:(hh + 1) * 128].unsqueeze(1).broadcast_to([128, 4, 128]),
                    scalar1=-1.0, scalar2=30000.0, op0=ALU.add, op1=ALU.mult), reads=[("sf", 0)], writes=["mbias"])
            S.op("dve", lambda: nc.vector.tensor_copy(out=identb[:], in_=ident[:]), reads=["ident"], writes=["identb"])
            S.op("dve", lambda: nc.vector.memset(uT[:, :, 0:32], 0.0), writes=["u"])
            S.op("dve", lambda: nc.vector.memset(sinkrow[:], 0.0), writes=["sinkrow"])
            S.op("dve", lambda: nc.vector.tensor_copy(out=sinkrow[0:1, :, :], in_=sinkexp[0:1, :].unsqueeze(2).broadcast_to([1, 8, 128])),
                 reads=["sinkexp"], writes=["sinkrow"])
            S.dma("sp", "cst", tabtP[:], ropetm[0:128, :, :], writes=["tabt"])

            def xkey(ti):
                return ("xb", ti % 2)

            def xsrc_of(ti):
                xb = ti % 2
                return lambda b: xbuf[:, xb, b, :]

            def load_xb(ti, b):
                xb = ti % 2
                r0 = ti * TT + b * 128
                S.dma("sp", "x%d%d" % (xb, b), xbuf[:, xb, b, :], xp[r0:r0 + 128, :], writes=[(xkey(ti), b)])

            def load_x(ti):
                xb = ti % 2
                for b in range(4):
                    load_xb(ti, b)

            def load_tab(ti):
                S.dma("sp", "tb", tab[:, :, :], ropefm[:, :, ti * TT:(ti + 1) * TT], writes=["tab"])

            def P1gen(ti):
                h, hk = HA[ti % 2]
                return norm_gen("p", xsrc_of(ti), 4, 128, 1, 0, xkey=xkey(ti), hdst=h, hkey=hk)

            def set_cur(ti):
                cur["h"], cur["hk"] = HA[ti % 2]

            def P2(ti, side=None):
                last = ti == NTILE - 1
                set_cur(ti)

                def v_evac_p(s, wv):
                    bk = nb_()
                    for b in range(4):
                        tm_mm(bk, b * 128, wv, s, list(range(8)), 0, 128, slice(b * 128, (b + 1) * 128), 128, True, True)
                    S.op("act", lambda: nc.scalar.copy(out=vtok[:, 1:5, :].rearrange("p a b -> p (a b)"), in_=ps[bk][:, :]), writes=[("ps", bk), "vcur"])
                    if last:
                        S.op("dve", lambda: nc.vector.tensor_copy(out=otokp[:, 128:256], in_=ps[bk][:, 384:512]), writes=[("ps", bk), OKEY])

                def u_dst_p(c, b0, i1):
                    S.op("dve", lambda: nc.vector.tensor_tensor(out=uT[:, c, 32:32 + TT], in0=ps[b0][:, :], in1=sf[:, i1, :], op=ALU.mult),
                         reads=[("sf", i1)], writes=[("ps", b0), "u"])

                extra = None
                if last:
                    extra = dict(tsl=slice(384, 512), M=128, tabtC=tabtP[:, 0, :], tabtS=tabtP[:, 1, :], k=otokp[:, 0:128], u=otokp[:, 256:768], okey=OKEY)
                phase2("p", TT, tab[:, 0, :], tab[:, 1, :], kT[:, 128:128 + TT], v_evac_p, u_dst_p, extra, side=side)
                if last:
                    S.dma("sp", "o", kwp, otokp[:, 0:128], reads=[OKEY])
                    S.dma("sp", "o", vwp, otokp[:, 128:256], reads=[OKEY])
                    S.dma("sp", "o", cvp, otokp[98:128, 256:768], reads=[OKEY])
                if ti + 1 < NTILE:
                    load_tab(ti + 1)

            def conv_p():
                return conv_taps(TT, lambda c, j: uT[:, c, 2 + j:2 + j + TT], lambda c: dwb[:, c, :])

            def halo_u():
                S.op("act", lambda: nc.scalar.copy(out=uT[:, :, 0:32], in_=uT[:, :, TT:TT + 32]), reads=["u"], writes=["u"])

            def P3(ti, side=None):
                def stageA(b, g):
                    gb = ti * 4 + b
                    has_prev = gb > 0
                    pbuf = g
                    for hh, kcol in ((1, 128 + b * 128), (0, b * 128)):
                        if hh == 0 and not has_prev:
                            continue
                        bk = nb_()
                        fns = [lambda hh=hh, bk=bk: nc.tensor.matmul(ps[bk][:, :], identb[:, :], mbias[:, hh, :, :].rearrange("p j q -> p (j q)"), start=True, stop=True)]
                        for j in range(4):
                            fns.append(lambda j=j, bk=bk, kcol=kcol: nc.tensor.matmul(
                                ps[bk][:, j * 128:(j + 1) * 128], kT[:, kcol:kcol + 128], qTz[g][:, j, b * 128:(b + 1) * 128],
                                start=False, stop=True, skip_group_check=True))
                        S.group("pe", fns, reads=["qT", "kTcur", "kThalo", "mbias", "identb"], writes=[("ps", bk)])
                        S.op("act", lambda hh=hh, bk=bk: nc.scalar.activation(out=PT[:, pbuf, hh, :], in_=ps[bk][:, :], func=AF.Exp, scale=SCALE),
                             writes=[("ps", bk), ("PT", pbuf, hh)])

                def stageB(b, g):
                    gb = ti * 4 + b
                    has_prev = gb > 0
                    pbuf = g
                    gs = slice(g * 64, (g + 1) * 64)
                    bO, bD = nb_(), nb_()
                    for bank, use_v in ((bO, True), (bD, False)):
                        fns = []
                        lc = vtok[:, 1 + b, gs] if use_v else onesb[:, :]
                        fns.append(lambda lc=lc, bank=bank: nc.tensor.matmul(ps[bank][0:64, :], lc, PT[:, pbuf, 1, :], start=True, stop=(use_v and not has_prev)))
                        if has_prev:
                            lp = vtok[:, b, gs] if use_v else onesb[:, :]
                            fns.append(lambda lp=lp, bank=bank: nc.tensor.matmul(ps[bank][0:64, :], lp, PT[:, pbuf, 0, :], start=False, stop=use_v))
                        if not use_v:
                            fns.append(lambda bank=bank: nc.tensor.matmul(ps[bank][0:64, :], onesb[:, :], sinkrow[:, g * 4:(g + 1) * 4, :].rearrange("p j q -> p (j q)"),
                                                                            start=False, stop=True))
                        S.group("pe", fns, reads=[("PT", pbuf, 0), ("PT", pbuf, 1), "vcur", "vhalo", "onesb", "sinkrow"], writes=[("ps", bank)])
                    ird = sfn()
                    S.op("act", lambda: nc.scalar.activation(out=sf[0:64, ird, :], in_=ps[bD][0:64, :], func=AF.Ln), writes=[("ps", bD), ("sf", ird)])
                    S.op("act", lambda: nc.scalar.activation(out=sf[0:64, ird, :], in_=sf[0:64, ird, :], func=AF.Exp, scale=-1.0), reads=[("sf", ird)], writes=[("sf", ird)])
                    S.op("dve", lambda: nc.vector.tensor_tensor(out=aT[0:64, g * 4:(g + 1) * 4, b * 128:(b + 1) * 128], in0=ps[bO][0:64, :].rearrange("p (j q) -> p j q", j=4),
                                                                in1=sf[0:64, ird, :].rearrange("p (j q) -> p j q", j=4), op=ALU.mult),
                         reads=[("sf", ird)], writes=[("ps", bO), "aT"])

                steps = [(b, g) for b in range(4) for g in range(2)]
                for k, (b, g) in enumerate(steps):
                    stageA(b, g)
                    if k > 0:
                        stageB(*steps[k - 1])
                    if side is not None:
                        next(side, None)
                stageB(*steps[-1])
                if side is not None:
                    for _ in side:
                        pass
                if ti + 1 < NTILE:
                    S.op("act", lambda: nc.scalar.copy(out=kT[:, 0:128], in_=kT[:, TT:TT + 128]), reads=["kTcur"], writes=["kThalo"])
                    S.op("act", lambda: nc.scalar.copy(out=vtok[:, 0, :], in_=vtok[:, 4, :]), reads=["vcur"], writes=["vhalo"])

            def finish_gen(ti):
                def after_block(b):
                    r0 = ti * TT + b * 128
                    S.dma("sp", "y%d%d" % (ti % 2, b), yp[r0:r0 + 128, :], xbuf[:, ti % 2, b, :], reads=[(xkey(ti), b)])
                    if ti + 2 < NTILE:
                        load_xb(ti + 2, b)
                for _ in final_gen(4, 128, xsrc_of(ti), xkey=xkey(ti), after_block=after_block):
                    yield

            load_x(0)
            load_tab(0)
            load_x(1)
            for _ in P1gen(0):
                pass
            P2(0)
            for ti in range(NTILE):
                more = ti + 1 < NTILE
                xk = xkey(ti)
                xs_ = xsrc_of(ti)
                def p3_side(ti=ti):
                    if ti == 0:
                        for j_, _ in enumerate(conv_p()):
                            if j_ % 6 == 5:
                                yield
                        halo_u()
                        yield
                        for _ in ln_gen(TT):
                            yield
                    if ti > 0:
                        for _ in finish_gen(ti - 1):
                            yield
                P3(ti, side=p3_side())
                set_cur(ti)
                merge(TT, side=(P1gen(ti + 1) if more else None))
                tm_proj_resid(4, 128, mT, "mT", "wo", 4, xs_, G1p, xkey=xk)
                n2 = norm_gen("p", xs_, 4, 128, 3, 2, xkey=xk, hdst=hB, hkey="hB")
                side = None
                if more:
                    P2(ti + 1, side=n2)

                    def ffn_side():
                        for _ in conv_p():
                            yield
                        halo_u()
                        for _ in ln_gen(TT):
                            yield
                    side = ffn_side()
                else:
                    for _ in n2:
                        pass
                ffn_in(TT, hsrc=hB, hkey="hB", side=side)
                tm_proj_resid(4, 128, hidT, "hidT", "fo", NHC // 2, xs_, G2p, xkey=xk, side=side)
                if KSTOP <= 6 + ti:
                    S.final_wait("sp")
                    return nc
            for _ in finish_gen(NTILE - 1):
                pass

        S.final_wait("sp")
        print("instructions", S.n_ins, "waits", S.n_wait, "sems", len(S.sems))
    return nc


def _host_consts():
    f32 = np.float32
    half = 8
    inv = (np.float32(500000.0) ** (-np.arange(0, 16, 2, dtype=f32) / f32(16))).astype(f32)
    pos = np.concatenate([np.arange(SEQ, dtype=f32), (PAST + np.tile(np.arange(ST), SB)).astype(f32)])
    ang = (pos[:, None] * inv[None, :]).astype(f32)
    cos, sin = np.cos(ang).astype(f32), np.sin(ang).astype(f32)
    ntok = pos.shape[0]
    Cd = np.ones((ntok, 64), f32)
    Sd = np.zeros((ntok, 64), f32)
    Cd[:, 0:8] = cos
    Cd[:, 8:16] = cos
    Sd[:, 0:8] = -sin
    Sd[:, 8:16] = sin
    C128 = np.concatenate([Cd, Cd], axis=1)
    S128 = np.concatenate([Sd, Sd], axis=1)
    ropefm = np.ascontiguousarray(np.stack([C128.T, S128.T], axis=1))
    tm_rows = np.concatenate([np.arange(SEQ - 128, SEQ), np.arange(SEQ, SEQ + NS_TOK)])
    ropetm = np.ascontiguousarray(np.stack([C128[tm_rows], S128[tm_rows]], axis=1))
    jj = np.arange(128)[:, None]
    ii = np.arange(128)[None, :]
    maskP = (jj > ii).astype(f32)
    maskC = (jj <= ii).astype(f32)
    t4 = np.arange(ST)
    msb = (jj > t4[None, :]).astype(f32)
    maskSb = np.tile(msb, (1, 128))
    kb, kt = np.arange(NS_TOK) // ST, np.arange(NS_TOK) % ST
    msn = ((kb[:, None] == kb[None, :]) & (kt[:, None] <= kt[None, :])).astype(f32)
    maskSn = np.zeros((128, 512), f32)
    m4 = np.broadcast_to(msn.reshape(NS_TOK, SB, 1, ST), (NS_TOK, SB, 4, ST)).reshape(NS_TOK, 256)
    maskSn[0:NS_TOK] = np.tile(m4, (1, 2))
    masks = np.ascontiguousarray(np.concatenate([maskP, maskC, maskSb, maskSn], axis=1))
    return ropefm, ropetm, masks, np.eye(128, dtype=f32)


def _ext_cols():
    def swp(base):
        return list(range(base + 8, base + 16)) + list(range(base, base + 8)) + list(range(base + 16, base + 64))
    cols = []
    for j in range(4):
        cols += list(range(j * 64, (j + 1) * 64)) + list(range((4 + j) * 64, (5 + j) * 64))
        cols += swp(j * 64) + swp((4 + j) * 64)
    cols += list(range(512, 640))
    cols += swp(512) + swp(576)
    cols += list(range(640, 768))
    for c in range(4):
        cols += list(range(768 + c * 128, 768 + (c + 1) * 128))
        cols += list(range(1280 + c * 128, 1280 + (c + 1) * 128))
    cols += list(range(1792, 2816))
    cols += list(range(2816, 3840))
    assert len(cols) == NEXT
    return np.array(cols)


_PROGRAM = None


def kernel(x_prompt, x_sample, state_k_win, state_v_win, state_conv, c_prompt, c_sample,
           norm1_g, norm2_g, w_ada, b_ada, w_in, sinks, w_attn_o, conv_w, conv_b,
           conv_ln_g, conv_ln_b, w_conv_o, w_out, w_ffn_in, w_ffn_out, final_norm_g):
    global _PROGRAM
    f32 = np.float32
    A = lambda a: np.ascontiguousarray(np.asarray(a, dtype=f32))
    if _PROGRAM is None:
        _PROGRAM = build_program()
    nc = _PROGRAM
    ropefm, ropetm, masks, ident = _host_consts()
    def pmaj(w, width_list, nk=8):
        n = w.shape[1]
        r = w.reshape(nk, 128, n).transpose(1, 0, 2)
        out, c0 = [], 0
        for wd in width_list:
            out.append(r[:, :, c0:c0 + wd].reshape(128, nk * wd))
            c0 += wd
        assert c0 == n
        return A(np.concatenate(out, axis=1))

    winx_full = np.asarray(w_in)[0][:, _ext_cols()]
    widths = [256] * 5 + [128] + [256] * 12
    winx = pmaj(winx_full, widths)
    wada_np = np.asarray(w_ada)[0]
    wadaf = pmaj(np.concatenate([wada_np[:, m * D:(m + 1) * D] for m in (0, 1, 3, 4)], axis=1), [256] * 16)
    wadat = A(np.concatenate([wada_np[:, 2 * D:3 * D], wada_np[:, 5 * D:6 * D]], axis=1))
    wfi_pm = pmaj(np.asarray(w_ffn_in)[0], [256] * 22)
    wao_np = np.asarray(w_attn_o)[0].reshape(8, 64, D).transpose(1, 0, 2)
    wao_pm = A(np.concatenate([wao_np[:, :, p * 256:(p + 1) * 256].reshape(64, 8 * 256) for p in range(4)], axis=1))
    wco_np = np.asarray(w_conv_o)[0].reshape(4, 128, D).transpose(1, 0, 2)
    wco_pm = A(np.concatenate([wco_np[:, :, p * 256:(p + 1) * 256].reshape(128, 4 * 256) for p in range(4)], axis=1))
    b_ = np.asarray(b_ada)[0]
    vec_rows = [np.asarray(norm1_g)[0].reshape(8, 128), np.asarray(norm2_g)[0].reshape(8, 128),
                b_[0:D].reshape(8, 128), b_[D:2 * D].reshape(8, 128), b_[3 * D:4 * D].reshape(8, 128), b_[4 * D:5 * D].reshape(8, 128),
                np.asarray(conv_b)[0].reshape(4, 128), np.asarray(conv_ln_g)[0].reshape(4, 128), np.asarray(conv_ln_b)[0].reshape(4, 128),
                np.asarray(conv_w)[0].reshape(CW * 4, 128)]
    vecs = A(np.concatenate(vec_rows, axis=0))
    assert vecs.shape == (NVEC, 128)
    shared = dict(wadaf=wadaf, wadat=wadat, bada=A(b_.reshape(1, -1)), winx=winx, wao=wao_pm,
                  wco=wco_pm, wout=A(np.asarray(w_out)[0]), wfi=wfi_pm,
                  wfo=A(np.asarray(w_ffn_out)[0]), vecs=vecs, fng=A(np.asarray(final_norm_g).reshape(1, D)),
                  sinks=A(np.asarray(sinks).reshape(1, 8)), ropefm=ropefm, ropetm=ropetm, masks=masks, identd=ident)
    xpn, xsn = np.asarray(x_prompt), np.asarray(x_sample)
    skn, svn, scn = np.asarray(state_k_win)[0], np.asarray(state_v_win)[0], np.asarray(state_conv)[0]
    cpn, csn = np.asarray(c_prompt), np.asarray(c_sample)
    in_maps = []
    for i in range(NCORE):
        sl = slice(i * SB, (i + 1) * SB)
        m = dict(shared)
        m.update(xp=A(xpn[i]), xs=A(xsn[sl].reshape(NS_TOK, D)), ksin=A(skn[sl].reshape(SB, 128, 128)),
                 vsin=A(svn[sl].reshape(SB, 128, 128)), csin=A(scn[sl]),
                 call=A(np.concatenate([cpn[i:i + 1], csn[sl]], axis=0)))
        in_maps.append(m)
    res = run_bass_kernel_spmd(nc, in_maps, core_ids=list(range(NCORE)))
    R = res.results
    y_prompt = np.stack([R[i]["yp"] for i in range(NCORE)], axis=0).astype(f32)
    y_sample = np.concatenate([R[i]["ys"].reshape(SB, ST, D) for i in range(NCORE)], axis=0).astype(f32)
    kwp = np.stack([R[i]["kwp"].reshape(128, 2, 64) for i in range(NCORE)], axis=0)[None].astype(f32)
    vwp = np.stack([R[i]["vwp"].reshape(128, 2, 64) for i in range(NCORE)], axis=0)[None].astype(f32)
    cvp = np.stack([R[i]["cvp"] for i in range(NCORE)], axis=0)[None].astype(f32)
    kws = np.concatenate([R[i]["kws"].reshape(SB, 128, 2, 64) for i in range(NCORE)], axis=0)[None].astype(f32)
    vws = np.concatenate([R[i]["vws"].reshape(SB, 128, 2, 64) for i in range(NCORE)], axis=0)[None].astype(f32)
    cvs = np.concatenate([R[i]["cvs"] for i in range(NCORE)], axis=0)[None].astype(f32)
    return (y_prompt, y_sample, kwp, vwp, cvp, kws, vws, cvs)
```
